# Optimizing a Trainium2 kernel written in Bass

```python
import math
import jax
import jax.numpy as jnp
from jax import lax
import numpy as np

D_MODEL = 1024
BATCH = 32
SEQ = 256
DEPTH = 2
DEC_BATCH = 4
DEC_SEQ = 1024
PAST_LEN = 256

GRID_W = 64
N_ATTN = 4
DH_ATTN = 64
W_ATTN = N_ATTN * 2 * DH_ATTN
N_FOUR = 4
DG_FOUR = 128
W_FOUR = N_FOUR * DG_FOUR
N_MLSTM = 4
DH_MLSTM = 128
W_MLSTM = N_MLSTM * DH_MLSTM
N_GATE = 4 * N_MLSTM
P_IN = 3 * W_ATTN + W_FOUR + 4 * W_MLSTM + N_GATE
SPLIT_AT = (W_ATTN, 2 * W_ATTN, 3 * W_ATTN, 3 * W_ATTN + W_FOUR,
            3 * W_ATTN + W_FOUR + W_MLSTM, 3 * W_ATTN + W_FOUR + 2 * W_MLSTM,
            3 * W_ATTN + W_FOUR + 3 * W_MLSTM, 3 * W_ATTN + W_FOUR + 4 * W_MLSTM)
N_BRANCH = 3
D_FF = 2816
N_MOD = 9
CHUNK = 64
Q_BLOCK = 128
ROPE_BASE = 10000.0
ROPE_AXIS_PAIRS = DH_ATTN // 4
ATTN_SCALE = DH_ATTN ** -0.5
MLSTM_K_SCALE = DH_MLSTM ** -0.5
EPS = 1e-6

kernel_name = 'hybrid_diffattn_fnet_mlstm_macaron_step'


def rms_norm(x, g):
    xf = x.astype(jnp.float32)
    y = xf * lax.rsqrt(jnp.mean(xf * xf, axis=-1, keepdims=True) + EPS)
    return (y * g.astype(jnp.float32)).astype(x.dtype)


def modulate(x, shift, scale):
    return x * (1 + scale[:, None, :]) + shift[:, None, :]


def adaln(cvec, w, b):
    m = jax.nn.silu(cvec) @ w + b
    return jnp.split(m, N_MOD, axis=-1)


def swiglu(u, w_in, w_out):
    a, g = jnp.split(u @ w_in, 2, axis=-1)
    return (jax.nn.silu(a) * g) @ w_out


def grid_rope(n_tok):
    rows = n_tok // GRID_W
    row = jnp.repeat(jnp.arange(rows, dtype=jnp.float32), GRID_W)
    col = (jnp.arange(n_tok) % GRID_W).astype(jnp.float32)
    inv = ROPE_BASE ** (-jnp.arange(ROPE_AXIS_PAIRS, dtype=jnp.float32) / ROPE_AXIS_PAIRS)
    ang = jnp.concatenate([row[:, None] * inv, col[:, None] * inv], axis=-1)
    return jnp.cos(ang), jnp.sin(ang)


def apply_rope(x, cos, sin):
    B, H, T, _ = x.shape
    xm = x.reshape(B, H, T, 2, DH_ATTN)
    c = cos[:, None, :].astype(x.dtype)
    s = sin[:, None, :].astype(x.dtype)
    half = DH_ATTN // 2
    x1, x2 = xm[..., :half], xm[..., half:]
    out = jnp.concatenate([x1 * c - x2 * s, x1 * s + x2 * c], axis=-1)
    return out.reshape(B, H, T, 2 * DH_ATTN)


def diff_lambda(lam_p, lam_init):
    lp = lam_p.astype(jnp.float32)
    return jnp.exp(jnp.sum(lp[0] * lp[1])) - jnp.exp(jnp.sum(lp[2] * lp[3])) + lam_init


def diff_attention(q, k, v, lam, lam_init, g_sub):
    B, H, Tq, _ = q.shape
    nb = Tq // Q_BLOCK
    k1, k2 = k[..., :DH_ATTN], k[..., DH_ATTN:]

    def block(qb):
        s1 = jnp.einsum('bhqd,bhkd->bhqk', qb[..., :DH_ATTN], k1).astype(jnp.float32) * ATTN_SCALE
        s2 = jnp.einsum('bhqd,bhkd->bhqk', qb[..., DH_ATTN:], k2).astype(jnp.float32) * ATTN_SCALE
        p = jax.nn.softmax(s1, axis=-1) - lam * jax.nn.softmax(s2, axis=-1)
        return jnp.einsum('bhqk,bhkd->bhqd', p.astype(v.dtype), v)

    qs = jnp.moveaxis(q.reshape(B, H, nb, Q_BLOCK, 2 * DH_ATTN), 2, 0)
    o = lax.map(block, qs)
    o = jnp.moveaxis(o, 0, 2).reshape(B, H, Tq, 2 * DH_ATTN)
    return rms_norm(o, g_sub) * (1.0 - lam_init)


def fourier_mix(z):
    B, T, _ = z.shape
    zg = z.reshape(B, T, N_FOUR, DG_FOUR).astype(jnp.float32)
    f = jnp.fft.fft2(zg, axes=(1, 3), norm='ortho').real
    return f.reshape(B, T, W_FOUR).astype(z.dtype)


def mlstm_scan(q, k, v, ig, lf, C0, n0, m0):
    B, H, T, D = q.shape
    nc = T // CHUNK

    def chunks(a):
        return jnp.moveaxis(a.reshape((B, H, nc, CHUNK) + a.shape[3:]), 2, 0)

    lower = jnp.tril(jnp.ones((CHUNK, CHUNK), dtype=bool))

    def step(carry, xs):
        C, n, m = carry
        qc, kc, vc, ic, fc = xs
        b = jnp.cumsum(fc, axis=-1)
        logw = jnp.where(lower, b[..., :, None] - b[..., None, :] + ic[..., None, :], -jnp.inf)
        prev = b + m[..., None]
        m_t = jnp.maximum(prev, jnp.max(logw, axis=-1))
        w = jnp.exp(logw - m_t[..., None])
        sp = jnp.exp(prev - m_t)
        s = jnp.einsum('bhtd,bhsd->bhts', qc, kc) * w
        num = sp[..., None] * jnp.einsum('bhvk,bhtk->bhtv', C, qc) + jnp.einsum('bhts,bhsv->bhtv', s, vc)
        den = sp * jnp.einsum('bhk,bhtk->bht', n, qc) + jnp.sum(s, axis=-1)
        h = num / jnp.maximum(jnp.abs(den), jnp.exp(-m_t))[..., None]
        m_new = m_t[..., -1]
        wl = jnp.exp(b[..., -1:] - b + ic - m_new[..., None])
        decay = jnp.exp(b[..., -1] + m - m_new)
        C_new = decay[..., None, None] * C + jnp.einsum('bhs,bhsv,bhsk->bhvk', wl, vc, kc)
        n_new = decay[..., None] * n + jnp.einsum('bhs,bhsk->bhk', wl, kc)
        return (C_new, n_new, m_new), h

    (C1, n1, m1), hs = lax.scan(step, (C0, n0, m0),
                                (chunks(q), chunks(k), chunks(v), chunks(ig), chunks(lf)))
    return jnp.moveaxis(hs, 0, 2).reshape(B, H, T, D), (C1, n1, m1)


def mlstm_bidirectional(q, k, v, ig_f, lf_f, ig_b, lf_b, C0, n0, m0):
    h_f, (Cf, nf, mf) = mlstm_scan(q, k, v, ig_f, lf_f, C0[:, 0], n0[:, 0], m0[:, 0])

    def rev(a):
        return jnp.flip(a, axis=2)

    h_b, (Cb, nb, mb) = mlstm_scan(rev(q), rev(k), rev(v), rev(ig_b), rev(lf_b),
                                   C0[:, 1], n0[:, 1], m0[:, 1])
    return h_f + rev(h_b), (jnp.stack([Cf, Cb], axis=1), jnp.stack([nf, nb], axis=1),
                            jnp.stack([mf, mb], axis=1))


def token_mixer(u, lp, lam_init, ctx):
    B, T, _ = u.shape
    z = u @ lp['w_in']
    za_q, za_k, za_v, zf, zm_q, zm_k, zm_v, zm_o, zg = jnp.split(z, SPLIT_AT, axis=-1)

    def attn_heads(a):
        return a.reshape(B, T, N_ATTN, 2 * DH_ATTN).transpose(0, 2, 1, 3)

    q, k, v = attn_heads(za_q), attn_heads(za_k), attn_heads(za_v)
    if ctx is None:
        q_use, k_use, v_use = q, k, v
    else:
        k_ctx, v_ctx, C0, n0, m0 = ctx
        cos, sin = grid_rope(T)
        q_use = apply_rope(q, cos, sin)
        k_use = jnp.concatenate([k_ctx.astype(k.dtype), apply_rope(k, cos, sin)], axis=2)
        v_use = jnp.concatenate([v_ctx.astype(v.dtype), v], axis=2)
    lam = diff_lambda(lp['attn_lambda'], lam_init)
    a_out = diff_attention(q_use, k_use, v_use, lam, lam_init, lp['g_attn_sub'])
    a_out = a_out.transpose(0, 2, 1, 3).reshape(B, T, W_ATTN)

    f_out = fourier_mix(zf)

    def mlstm_heads(a):
        return a.reshape(B, T, N_MLSTM, DH_MLSTM).transpose(0, 2, 1, 3).astype(jnp.float32)

    mq, mk, mv = mlstm_heads(zm_q), mlstm_heads(zm_k) * MLSTM_K_SCALE, mlstm_heads(zm_v)
    gates = (zg + lp['b_mgate']).astype(jnp.float32).reshape(B, T, 4, N_MLSTM).transpose(2, 0, 3, 1)
    ig_f, lf_f = gates[0], jax.nn.log_sigmoid(gates[1])
    ig_b, lf_b = gates[2], jax.nn.log_sigmoid(gates[3])
    if ctx is None:
        C0 = jnp.zeros((B, 2, N_MLSTM, DH_MLSTM, DH_MLSTM), jnp.float32)
        n0 = jnp.zeros((B, 2, N_MLSTM, DH_MLSTM), jnp.float32)
        m0 = jnp.zeros((B, 2, N_MLSTM), jnp.float32)
    h, (C1, n1, m1) = mlstm_bidirectional(mq, mk, mv, ig_f, lf_f, ig_b, lf_b,
                                          C0.astype(jnp.float32), n0.astype(jnp.float32),
                                          m0.astype(jnp.float32))
    h = rms_norm(h, lp['g_mlstm']).transpose(0, 2, 1, 3).reshape(B, T, W_MLSTM).astype(u.dtype)
    m_out = h * jax.nn.sigmoid(zm_o)

    gb = jax.nn.sigmoid(u @ lp['w_branch_gate']).reshape(B, T, N_BRANCH, D_MODEL)
    merged = (gb[:, :, 0] * (a_out @ lp['w_br_attn'])
              + gb[:, :, 1] * (f_out @ lp['w_br_four'])
              + gb[:, :, 2] * (m_out @ lp['w_br_mlstm']))
    out = merged @ lp['w_out']
    if ctx is None:
        dt = u.dtype
        return out, (k, v, C1.astype(dt), n1.astype(dt), m1.astype(dt))
    return out, None


def trunk_layer(x, mods, lp, lam_init, ctx):
    sh1, sc1, gt1, sh2, sc2, gt2, sh3, sc3, gt3 = mods
    g = lp['g_norm']
    u = modulate(rms_norm(x, g[0]), sh1, sc1)
    x = x + 0.5 * gt1[:, None, :] * swiglu(u, lp['w_ffn1_in'], lp['w_ffn1_out'])
    u = modulate(rms_norm(x, g[1]), sh2, sc2)
    y, ctx_out = token_mixer(u, lp, lam_init, ctx)
    x = x + gt2[:, None, :] * y
    u = modulate(rms_norm(x, g[2]), sh3, sc3)
    x = x + 0.5 * gt3[:, None, :] * swiglu(u, lp['w_ffn2_in'], lp['w_ffn2_out'])
    return x, ctx_out


def setup_inputs(seed: int = 0) -> dict:
    key = jax.random.key(seed)
    ks = jax.random.split(key, 32)
    D = D_MODEL

    def nrm(k, shape, scale):
        return scale * jax.random.normal(k, shape, jnp.float32)

    is_forget = jnp.array([False, True, False, True])
    b_in_gate = nrm(ks[16], (DEPTH, 4, N_MLSTM), 0.1)
    b_fg_gate = 3.0 + 3.0 * jax.random.uniform(ks[17], (DEPTH, 4, N_MLSTM), jnp.float32)
    b_mgate = jnp.where(is_forget[None, :, None], b_fg_gate, b_in_gate).reshape(DEPTH, N_GATE)
    return {
        'x_prompt': nrm(ks[0], (BATCH, SEQ, D), 1.0),
        'x_sample': nrm(ks[1], (DEC_BATCH, DEC_SEQ, D), 1.0),
        'cache_k': nrm(ks[2], (DEC_BATCH, DEPTH, N_ATTN, PAST_LEN, 2 * DH_ATTN), 1.0),
        'cache_v': nrm(ks[3], (DEC_BATCH, DEPTH, N_ATTN, PAST_LEN, 2 * DH_ATTN), 1.0),
        'state_C': nrm(ks[4], (DEC_BATCH, DEPTH, 2, N_MLSTM, DH_MLSTM, DH_MLSTM), 0.3),
        'state_n': nrm(ks[5], (DEC_BATCH, DEPTH, 2, N_MLSTM, DH_MLSTM), 0.3),
        'state_m': nrm(ks[6], (DEC_BATCH, DEPTH, 2, N_MLSTM), 1.0),
        'c': nrm(ks[7], (DEC_BATCH, D), 1.0),
        'c_ctx': nrm(ks[8], (D,), 1.0),
        'w_ada': nrm(ks[9], (DEPTH, D, N_MOD * D), 0.5 * D ** -0.5),
        'b_ada': nrm(ks[10], (DEPTH, N_MOD * D), 0.02),
        'g_norm': 1.0 + nrm(ks[11], (DEPTH, 3, D), 0.02),
        'w_ffn1_in': nrm(ks[12], (DEPTH, D, 2 * D_FF), D ** -0.5),
        'w_ffn1_out': nrm(ks[13], (DEPTH, D_FF, D), D_FF ** -0.5),
        'w_ffn2_in': nrm(ks[14], (DEPTH, D, 2 * D_FF), D ** -0.5),
        'w_ffn2_out': nrm(ks[15], (DEPTH, D_FF, D), D_FF ** -0.5),
        'w_in': nrm(ks[18], (DEPTH, D, P_IN), D ** -0.5),
        'b_mgate': b_mgate,
        'attn_lambda': nrm(ks[19], (DEPTH, 4, DH_ATTN), 0.1),
        'g_attn_sub': 1.0 + nrm(ks[20], (DEPTH, 2 * DH_ATTN), 0.02),
        'g_mlstm': 1.0 + nrm(ks[21], (DEPTH, DH_MLSTM), 0.02),
        'w_branch_gate': nrm(ks[22], (DEPTH, D, N_BRANCH * D), D ** -0.5),
        'w_br_attn': nrm(ks[23], (DEPTH, W_ATTN, D), W_ATTN ** -0.5),
        'w_br_four': nrm(ks[24], (DEPTH, W_FOUR, D), W_FOUR ** -0.5),
        'w_br_mlstm': nrm(ks[25], (DEPTH, W_MLSTM, D), W_MLSTM ** -0.5),
        'w_out': nrm(ks[26], (DEPTH, D, D), D ** -0.5),
        'g_final': 1.0 + nrm(ks[27], (D,), 0.02),
    }


def reference(x_prompt, x_sample, cache_k, cache_v, state_C, state_n, state_m, c, c_ctx,
              w_ada, b_ada, g_norm, w_ffn1_in, w_ffn1_out, w_ffn2_in, w_ffn2_out, w_in, b_mgate,
              attn_lambda, g_attn_sub, g_mlstm, w_branch_gate, w_br_attn, w_br_four, w_br_mlstm,
              w_out, g_final):
    hp = x_prompt
    hs = x_sample
    ks_l, vs_l, Cs_l, ns_l, ms_l = [], [], [], [], []
    for l in range(DEPTH):
        lam_init = 0.8 - 0.6 * math.exp(-0.3 * l)
        lp = {
            'g_norm': g_norm[l], 'w_ffn1_in': w_ffn1_in[l], 'w_ffn1_out': w_ffn1_out[l],
            'w_ffn2_in': w_ffn2_in[l], 'w_ffn2_out': w_ffn2_out[l], 'w_in': w_in[l],
            'b_mgate': b_mgate[l], 'attn_lambda': attn_lambda[l], 'g_attn_sub': g_attn_sub[l],
            'g_mlstm': g_mlstm[l], 'w_branch_gate': w_branch_gate[l], 'w_br_attn': w_br_attn[l],
            'w_br_four': w_br_four[l], 'w_br_mlstm': w_br_mlstm[l], 'w_out': w_out[l],
        }
        mods_ctx = adaln(c_ctx[None, :], w_ada[l], b_ada[l])
        mods_lat = adaln(c, w_ada[l], b_ada[l])
        hp, (k_l, v_l, C_l, n_l, m_l) = trunk_layer(hp, mods_ctx, lp, lam_init, None)
        ctx_l = (cache_k[:, l], cache_v[:, l], state_C[:, l], state_n[:, l], state_m[:, l])
        hs, _ = trunk_layer(hs, mods_lat, lp, lam_init, ctx_l)
        ks_l.append(k_l)
        vs_l.append(v_l)
        Cs_l.append(C_l)
        ns_l.append(n_l)
        ms_l.append(m_l)
    y_prompt = rms_norm(hp, g_final)
    y_sample = rms_norm(hs, g_final)
    new_cache_k = jnp.stack(ks_l, axis=1)
    new_cache_v = jnp.stack(vs_l, axis=1)
    new_state_C = jnp.stack(Cs_l, axis=1)
    new_state_n = jnp.stack(ns_l, axis=1)
    new_state_m = jnp.stack(ms_l, axis=1)
    return (y_prompt, y_sample, new_cache_k, new_cache_v, new_state_C, new_state_n, new_state_m)
```

```python
import math
from contextlib import ExitStack, contextmanager
import numpy as np
import concourse.bass as bass
import concourse.mybir as mybir
from concourse.bass_utils import run_bass_kernel_spmd

F32 = mybir.dt.float32
BF16 = mybir.dt.bfloat16
AF = mybir.ActivationFunctionType
ALU = mybir.AluOpType
AX = mybir.AxisListType

D = 1024
DEPTH = 2
NCORES = 8
DFF = 2816
NJ = DFF // 128
P_IN = 4112
EPS = 1e-6
K_SCALE = 128 ** -0.5
import os
DBG_BR = os.environ.get("DBG_BR", "mafg")
DBG_AT = int(os.environ.get("DBG_AT", "9"))
DBG_TM = int(os.environ.get("DBG_TM", "7"))


class Buf:
    def __init__(self, name, t):
        self.name = name
        self.t = t
        self.st = {}

    def __getitem__(self, idx):
        return self.t[idx]


class K:
    ENG = ("pe", "act", "dve", "pool", "sp")

    def __init__(self, nc, root):
        self.nc = nc
        self.root = root
        self.es = root
        self.eng = {"pe": nc.tensor, "act": nc.scalar, "dve": nc.vector, "pool": nc.gpsimd, "sp": nc.sync}
        self.sem = {}
        self.tick = {}
        self.seen = {e: {} for e in self.ENG}
        for e in self.ENG:
            self.sem[e] = root.enter_context(nc.semaphore("tk_" + e))
            self.tick[e] = 0
        self.dsem = {}
        self.free_sems = {}
        self.all_dsems = []
        self.scope_bufs = [[]]
        self.ctr = 0
        self.n_ins = 0
        self.n_wait = 0

    def sb(self, name, shape, dtype=F32):
        self.ctr += 1
        t = self.es.enter_context(self.nc.sbuf_tensor("%s_%d" % (name, self.ctr), list(shape), dtype))
        b = Buf(name, t)
        self.scope_bufs[-1].append(b)
        self.hw = max(getattr(self, "hw", 0), self.nc.sbuf_base)
        return b

    def ps(self, name, shape, dtype=F32):
        t = self.es.enter_context(self.nc.psum_tensor(name, list(shape), dtype))
        return Buf(name, t)

    @contextmanager
    def scope(self):
        old = self.es
        with ExitStack() as sub:
            self.es = sub
            self.scope_bufs.append([])
            yield
            self.barrier()
            for b in self.scope_bufs.pop():
                for qt in ("sw", "hw"):
                    ent = self.dsem.pop((id(b), qt), None)
                    if ent is not None:
                        self.free_sems.setdefault(qt, []).append(ent)
            self.es = old

    @staticmethod
    def _norm(x):
        return (x, None) if isinstance(x, Buf) else x

    @staticmethod
    def _states(buf, key):
        if key is None:
            return list(buf.st.values())
        out = []
        if None in buf.st:
            out.append(buf.st[None])
        if key in buf.st:
            out.append(buf.st[key])
        return out

    def _deps(self, reads, writes):
        deps = {}

        def add(tok):
            if tok is None:
                return
            s, v = tok
            if deps.get(id(s), (None, -1))[1] < v:
                deps[id(s)] = (s, v)

        for r in reads:
            b, key = self._norm(r)
            for st in self._states(b, key):
                add(st[0])
        for w in writes:
            b, key = self._norm(w)
            for st in self._states(b, key):
                add(st[0])
                for tok in st[1].values():
                    add(tok)
        return deps

    def _record(self, reads, writes, tok):
        for r in reads:
            b, key = self._norm(r)
            st = b.st.setdefault(key, [None, {}])
            st[1][id(tok[0])] = tok
        for w in writes:
            b, key = self._norm(w)
            if key is None:
                b.st = {None: [tok, {}]}
            else:
                b.st[key] = [tok, {}]

    def _emit_waits(self, e, deps):
        eng = self.eng[e]
        seen = self.seen[e]
        for sid, (s, v) in deps.items():
            if e == "pe" and s is self.sem["pe"]:
                continue
            if seen.get(sid, -1) >= v:
                continue
            eng.wait_ge(s, v)
            seen[sid] = v
            self.n_wait += 1

    def op(self, e, fn, reads=(), writes=()):
        return self.group(e, [fn], reads, writes)

    def group(self, e, fns, reads=(), writes=()):
        deps = self._deps(reads, writes)
        self._emit_waits(e, deps)
        ins = None
        for fn in fns:
            ins = fn(self.eng[e])
        self.tick[e] += 1
        ins.then_inc(self.sem[e], 1)
        self._record(reads, writes, (self.sem[e], self.tick[e]))
        self.n_ins += len(fns)
        return ins

    def dma(self, q, out, in_, sem, reads=(), writes=(), **kw):
        if isinstance(sem, str):
            sem = self._norm(list(writes)[0])[0]
        deps = self._deps(reads, writes)
        self._emit_waits(q, deps)
        qt = "sw" if q == "pool" else "hw"
        skey = (id(sem), qt)
        if skey not in self.dsem:
            fl = self.free_sems.setdefault(qt, [])
            if fl:
                self.dsem[skey] = fl.pop()
            else:
                ent = [self.root.enter_context(self.nc.semaphore("d%s_%d" % (qt, len(self.all_dsems)))), 0,
                       id(sem) in getattr(self, "nobar", ())]
                self.all_dsems.append(ent)
                self.dsem[skey] = ent
        ent = self.dsem[skey]
        ins = self.eng[q].dma_start(out=out, in_=in_, **kw)
        ent[1] += 16
        ins.then_inc(ent[0], 16)
        self._record(reads, writes, (ent[0], ent[1]))
        self.n_ins += 1
        return ins

    def barrier(self):
        toks = [(self.sem[e], self.tick[e]) for e in self.ENG if self.tick[e] > 0]
        toks += [(ent[0], ent[1]) for ent in self.all_dsems if ent[1] > 0 and not (len(ent) > 2 and ent[2])]
        for e in self.ENG:
            self._emit_waits(e, {id(s): (s, v) for (s, v) in toks if s is not self.sem[e]})


def _consts():
    c = {}
    c["ident"] = np.eye(128, dtype=np.float32)
    p = np.arange(128)
    first = (p % 64) < 32
    partner = np.where(first, p + 32, p - 32)
    perm = np.zeros((128, 128), np.float32)
    perm[partner, p] = 1.0
    c["perm"] = perm
    t = np.arange(1024)
    row = (t // 64).astype(np.float32)
    col = (t % 64).astype(np.float32)
    inv = (np.float32(10000.0) ** (-np.arange(16, dtype=np.float32) / np.float32(16))).astype(np.float32)
    ang = np.concatenate([row[:, None] * inv, col[:, None] * inv], axis=-1).astype(np.float32)
    cosv = np.cos(ang).astype(np.float32)
    sinv = np.sin(ang).astype(np.float32)
    i = p % 32
    c["cosT"] = np.ascontiguousarray(cosv[:, i].T)
    c["sinT"] = np.ascontiguousarray((sinv[:, i] * np.where(first, -1.0, 1.0)[None, :]).T.astype(np.float32))
    s = np.arange(128)
    c["triu"] = (s[:, None] <= s[None, :]).astype(np.float32)
    c["tril"] = (s[:, None] >= s[None, :]).astype(np.float32)
    c["mcol4"] = np.eye(4, dtype=np.float32)
    for T in (256, 1024):
        tt = np.arange(T, dtype=np.float64)
        a = 2.0 * np.pi * ((tt[:, None] * tt[None, :]) % T) / T
        c["dftc%d" % T] = np.cos(a).astype(np.float32)
        c["dftns%d" % T] = (-np.sin(a)).astype(np.float32)
    dd = np.arange(128, dtype=np.float64)
    a = 2.0 * np.pi * ((dd[:, None] * dd[None, :]) % 128) / 128
    c["dftd"] = np.concatenate([np.cos(a), np.sin(a)], axis=1).astype(np.float32)
    return c


WEIGHT_NAMES = ["w_ada", "b_ada", "g_norm", "w_ffn1_in", "w_ffn1_out", "w_ffn2_in", "w_ffn2_out", "w_in",
                "b_mgate", "attn_lambda", "g_attn_sub", "g_mlstm", "w_branch_gate", "w_br_attn", "w_br_four",
                "w_br_mlstm", "w_out", "g_final"]
WEIGHT_SHAPES = {
    "w_ada": [2, 1024, 9216], "b_ada": [2, 9216], "g_norm": [2, 3, 1024], "w_ffn1_in": [2, 1024, 5632],
    "w_ffn1_out": [2, 2816, 1024], "w_ffn2_in": [2, 1024, 5632], "w_ffn2_out": [2, 2816, 1024],
    "w_in": [2, 1024, 4112], "b_mgate": [2, 16], "attn_lambda": [2, 4, 64], "g_attn_sub": [2, 128],
    "g_mlstm": [2, 128], "w_branch_gate": [2, 1024, 3072], "w_br_attn": [2, 512, 1024],
    "w_br_four": [2, 512, 1024], "w_br_mlstm": [2, 512, 1024], "w_out": [2, 1024, 1024], "g_final": [1024],
}


def build_program(stage=99, nlayers=DEPTH):
    nc = bass.Bass("TRN2", target_bir_lowering=False)
    cst = _consts()
    with ExitStack() as root:
        k = K(nc, root)

        def din(name, shape):
            return nc.dram_tensor(name, list(shape), F32, kind="ExternalInput").ap()

        def dout(name, shape):
            return nc.dram_tensor(name, list(shape), F32, kind="ExternalOutput").ap()

        xp_d = din("xp", [1024, 1024])
        xs_d = din("xs", [1024, 1024])
        ck_d = din("ck", [2, 4, 256, 128])
        cv_d = din("cv", [2, 4, 256, 128])
        sC_d = din("sC", [2, 2, 4, 128, 128])
        sn_d = din("sn", [2, 2, 4, 128])
        sm_d = din("sm", [2, 2, 4])
        cvec_d = din("cvec", [2, 1024])
        W = {n: din(n, WEIGHT_SHAPES[n]) for n in WEIGHT_NAMES}
        C = {n: din("c_" + n, list(v.shape)) for n, v in cst.items()}
        yp_d = dout("yp", [1024, 1024])
        ys_d = dout("ys", [1024, 1024])
        nk_d = dout("nk", [4, 2, 4, 256, 128])
        nv_d = dout("nv", [4, 2, 4, 256, 128])
        nC_d = dout("nC", [4, 2, 2, 4, 128, 128])
        nn_d = dout("nn", [4, 2, 2, 4, 128])
        nm_d = dout("nm", [4, 2, 2, 4])
        outs = Buf("outs", None)
        octr = [0]

        def store(out_ap, in_ap, src, sem=None, q="sp", **kw):
            if os.environ.get("DBG_NOSTORE") and out_ap.tensor.name in os.environ["DBG_NOSTORE"].split(","):
                return
            octr[0] += 1
            k.dma(q, out_ap, in_ap, src, reads=[src], writes=[(outs, octr[0])], **kw)

        dumps = {}

        def dump(name, buf, ap, shape):
            if not os.environ.get("DBG_DUMP") or name in dumps:
                return
            dumps[name] = dout("dump_" + name, shape)
            octr[0] += 1
            k.dma("pool", dumps[name], ap, buf, reads=[buf], writes=[(outs, octr[0])])

        xT = k.sb("xT", [128, 8, 2048])
        ident = k.sb("ident", [128, 128])
        ones_f = k.sb("ones_f", [128, 128])
        ones_b = k.sb("ones_b", [128, 128], BF16)
        epsc = k.sb("epsc", [128, 1])
        onec = k.sb("onec", [128, 1])
        triu_b = k.sb("triu_b", [128, 128], BF16)
        tril_b = k.sb("tril_b", [128, 128], BF16)
        mcol4 = k.sb("mcol4", [4, 4])
        sel = k.sb("sel", [4, 4, 128])
        gs = k.sb("gs", [128, 2, 3, 8, 2])
        sh = k.sb("sh", [128, 2, 3, 8, 2])
        gt = k.sb("gt", [128, 2, 3, 8, 2])
        gsubc = k.sb("gsubc", [128, 2])
        gmc = k.sb("gmc", [128, 2])
        nlam = k.sb("nlam", [128, 2])
        bmg = k.sb("bmg", [128, 2, 16])
        scr = [k.sb("scr%d" % i, [128, 512]) for i in range(8)]
        sci = [0]

        def S():
            sci[0] = (sci[0] + 1) % len(scr)
            return scr[sci[0]]

        banks = [k.ps("bank%d" % i, [128, 512]) for i in range(8)]
        bi = [0]

        nrot = [6]

        def B():
            bi[0] = (bi[0] + 1) % nrot[0]
            return banks[bi[0]]

        ACC0, ACC1 = banks[6], banks[7]
        ACCP = [(banks[6], banks[7]), (banks[4], banks[5])]

        k.dma("sp", ident[:], C["ident"][:, :], "c0", writes=[ident])
        k.dma("pool", triu_b[:], C["triu"][:, :], "c1", writes=[triu_b])
        k.dma("pool", tril_b[:], C["tril"][:, :], "c1", writes=[tril_b])
        k.dma("sp", mcol4[:], C["mcol4"][:, :], "c0", writes=[mcol4])
        k.op("dve", lambda e: e.memset(ones_f[:], 1.0), writes=[ones_f])
        k.op("dve", lambda e: e.memset(ones_b[:], 1.0), writes=[ones_b])
        k.op("dve", lambda e: e.memset(epsc[:], EPS), writes=[epsc])
        k.op("dve", lambda e: e.memset(onec[:], 1.0), writes=[onec])
        for h in range(4):
            k.op("dve", lambda e, h=h: e.tensor_scalar(sel[:, h, :], ones_f[0:4, :], mcol4[:, h:h + 1], None, op0=ALU.mult),
                 reads=[ones_f, mcol4], writes=[(sel, h)])
        k.dma("sp", bmg[:].rearrange("p l g -> p (l g)"), W["b_mgate"].rearrange("l g -> (l g)").partition_broadcast(128),
              "c0", writes=[bmg])

        def mm(out_ap, pairs, reads, writes, tr=False):
            n = len(pairs)
            fns = []
            for i, (a, b) in enumerate(pairs):
                fns.append(lambda e, a=a, b=b, i=i: e.matmul(out_ap, a, b, start=(i == 0), stop=(i == n - 1)))
            k.group("pe", fns, reads, writes)

        def mm_acc(out_ap, a, b, start, stop, reads, writes):
            k.op("pe", lambda e: e.matmul(out_ap, a, b, start=start, stop=stop), reads, writes)

        def transp(out_ap, in_ap, idn_ap, reads, writes):
            k.op("pe", lambda e: e.transpose(out_ap, in_ap, idn_ap), reads, writes)

        with k.scope():
            xin = [k.sb("xin%d" % i, [128, 1024]) for i in range(2)]
            for c in range(16):
                src = xp_d if c < 8 else xs_d
                r0 = (c % 8) * 128
                xi = xin[c % 2]
                k.dma("sp", xi[:], src[r0:r0 + 128, :], "xin%d" % (c % 2), writes=[xi])
                for g in range(2):
                    bk = B()
                    for j in range(4):
                        kc = 4 * g + j
                        transp(bk[:, j * 128:(j + 1) * 128], xi[:, kc * 128:(kc + 1) * 128], ident[:], [xi, ident], [(bk, j)])
                    k.op("act", lambda e, bk=bk, g=g, c=c: e.copy(
                        xT[:, 4 * g:4 * g + 4, c * 128:(c + 1) * 128], bk[:].rearrange("p (j t) -> p j t", j=4)),
                        reads=[bk], writes=[(xT, (4 * g + j, c // 4)) for j in range(4)])

        scT = k.sb("scT", [128, 8, 2], BF16)
        gT = k.sb("gT", [128, 6, 8])
        bT = k.sb("bT", [128, 2, 72])
        modsL = k.sb("modsL", [128, 72, 2])
        wbufs = [k.sb("wbuf%d" % i, [128, 8, 512], BF16) for i in range(3)]
        wbi = [0]
        k.nobar = set(id(b) for b in wbufs)

        pref = {}

        def wprefetch(key, parts):
            if key not in pref:
                pref[key] = wload(parts)

        def wload(parts, key=None):
            if key is not None and key in pref:
                return pref.pop(key)
            wbi[0] = (wbi[0] + 1) % len(wbufs)
            w_ = wbufs[wbi[0]]
            for (cs, ap) in parts:
                k.dma("pool", w_[:, :, cs], ap.rearrange("(k p) n -> p k n", p=128), "wb", writes=[w_])
            return w_

        bgq = []
        bgdone = set()

        def bg_step(n=1):
            for _ in range(n):
                if not bgq:
                    return
                tag, fn = bgq.pop(0)
                fn()
                bgdone.add(tag)

        def bg_ensure(tag):
            while tag not in bgdone and bgq:
                bg_step()

        def P_ffn(l, which, pc):
            w_in = W["w_ffn%d_in" % which]
            return ("ffn%d_%d_%d" % (which, l, pc),
                    [(slice(0, 256), w_in[l, :, pc * 256:(pc + 1) * 256]),
                     (slice(256, 512), w_in[l, :, DFF + pc * 256:DFF + (pc + 1) * 256])])

        def P_mg(l):
            return ("mg_%d" % l, [(slice(0, 16), W["w_in"][l, :, 4096:4112])])

        def P_mq(l, hd):
            return ("mq_%d_%d" % (l, hd),
                    [(slice(0, 128), W["w_in"][l, :, 2048 + hd * 128:2048 + (hd + 1) * 128]),
                     (slice(128, 256), W["w_in"][l, :, 2560 + hd * 128:2560 + (hd + 1) * 128]),
                     (slice(256, 384), W["w_in"][l, :, 3072 + hd * 128:3072 + (hd + 1) * 128]),
                     (slice(384, 512), W["w_in"][l, :, 3584 + hd * 128:3584 + (hd + 1) * 128])])

        def P_at(l):
            return ("at_%d" % l, [(slice(0, 512), W["w_in"][l, :, 0:512])])

        def P_fn(l):
            return ("fn_%d" % l, [(slice(0, 512), W["w_in"][l, :, 1536:2048])])

        def P_me(l, m):
            return ("me_%d_%d" % (l, m),
                    [(slice(br * 128, (br + 1) * 128), W["w_branch_gate"][l, :, br * 1024 + m * 128:br * 1024 + (m + 1) * 128])
                     for br in range(3)])

        def mods_piece(l, pc):
            w_ = wload([(slice(0, 512), W["w_ada"][l, :, pc * 512:(pc + 1) * 512])])
            bk = B()
            for jj in range(4):
                mm(bk[:, 2 * jj:2 * jj + 2], [(w_[:, kc, jj * 128:(jj + 1) * 128], scT[:, kc, :]) for kc in range(8)],
                   [w_, scT], [(bk, jj)])
            k.op("dve", lambda e: e.tensor_copy(modsL[:, pc * 4:(pc + 1) * 4, :], bk[:, 0:8].rearrange("p (j c) -> p j c", c=2)),
                 reads=[bk], writes=[(modsL, pc)])

        def mods_fin(l, i):
            for cnd in range(2):
                k.op("dve", lambda e: e.tensor_tensor(modsL[:, 24 * i:24 * i + 24, cnd], modsL[:, 24 * i:24 * i + 24, cnd],
                                                      bT[:, l, 24 * i:24 * i + 24], ALU.add), reads=[modsL, bT], writes=[modsL])
            for cnd in range(2):
                k.op("dve", lambda e: e.tensor_copy(sh[:, l, i, :, cnd], modsL[:, (3 * i) * 8:(3 * i) * 8 + 8, cnd]),
                     reads=[modsL], writes=[(sh, (l, i))])
                k.op("dve", lambda e: e.scalar_tensor_tensor(
                    gs[:, l, i, :, cnd], modsL[:, (3 * i + 1) * 8:(3 * i + 1) * 8 + 8, cnd], 1.0, gT[:, l * 3 + i, :],
                    op0=ALU.add, op1=ALU.mult), reads=[modsL, gT], writes=[(gs, (l, i))])
                k.op("dve", lambda e: e.tensor_scalar(
                    gt[:, l, i, :, cnd], modsL[:, (3 * i + 2) * 8:(3 * i + 2) * 8 + 8, cnd], (1.0 if i == 1 else 0.5), None,
                    op0=ALU.mult), reads=[modsL], writes=[(gt, (l, i))])

        if stage >= 1:
            with k.scope():
                cT = k.sb("cT", [128, 8, 2])
                alb = k.sb("alb", [128, 2, 4, 64])
                for cnd in range(2):
                    k.dma("sp", cT[:, :, cnd], cvec_d[cnd, :].rearrange("(k p) -> p k", p=128), "c0", writes=[cT],
                          allow_slow_non_contiguous=True)
                k.dma("sp", gT[:], W["g_norm"].rearrange("l i (k p) -> p (l i) k", p=128), "c0", writes=[gT],
                      allow_slow_non_contiguous=True)
                k.dma("sp", bT[:], W["b_ada"].rearrange("l (j p) -> p l j", p=128), "c0", writes=[bT],
                      allow_slow_non_contiguous=True)
                k.dma("sp", gsubc[:], W["g_attn_sub"].rearrange("l p -> p l"), "c0", writes=[gsubc], allow_slow_non_contiguous=True)
                k.dma("sp", gmc[:], W["g_mlstm"].rearrange("l p -> p l"), "c0", writes=[gmc], allow_slow_non_contiguous=True)
                k.dma("sp", alb[:].rearrange("p l a b -> p (l a b)"),
                      W["attn_lambda"].rearrange("l a b -> (l a b)").partition_broadcast(128), "c0", writes=[alb])
                k.op("act", lambda e: e.activation(scT[:], cT[:], AF.Silu), reads=[cT], writes=[scT])
                for l in range(nlayers):
                    lam_init = 0.8 - 0.6 * math.exp(-0.3 * l)
                    t1 = S(); t2 = S()
                    for q in range(2):
                        k.op("dve", lambda e, q=q: e.tensor_tensor(t1[:, 0:64], alb[:, l, 2 * q, :], alb[:, l, 2 * q + 1, :], ALU.mult),
                             reads=[alb], writes=[t1])
                        k.op("dve", lambda e, q=q: e.reduce_sum(t2[:, q:q + 1], t1[:, 0:64], AX.X), reads=[t1], writes=[t2])
                    k.op("act", lambda e: e.activation(t2[:, 2:4], t2[:, 0:2], AF.Exp), reads=[t2], writes=[t2])
                    k.op("dve", lambda e: e.tensor_tensor(t2[:, 4:5], t2[:, 3:4], t2[:, 2:3], ALU.subtract), reads=[t2], writes=[t2])
                    k.op("dve", lambda e: e.tensor_scalar(nlam[:, l:l + 1], t2[:, 4:5], -lam_init, None, op0=ALU.add),
                         reads=[t2], writes=[nlam])
                    k.op("dve", lambda e: e.tensor_scalar(gsubc[:, l:l + 1], gsubc[:, l:l + 1], 1.0 - lam_init, None, op0=ALU.mult),
                         reads=[gsubc], writes=[gsubc])
            for l in range(nlayers):
                for i in range(3):
                    for pc in range(6 * i, 6 * i + 6):
                        bgq.append((("p", l, pc), (lambda l=l, pc=pc: mods_piece(l, pc))))
                    bgq.append((("fin", l, i), (lambda l=l, i=i: mods_fin(l, i))))

        def rstd_from_psum(st_ap, width, nfeat, srcs):
            a = S()
            k.op("act", lambda e: e.activation(a[:, 0:width], st_ap, AF.Ln, bias=epsc[:], scale=1.0 / nfeat),
                 reads=srcs + [epsc], writes=[a])
            k.op("act", lambda e: e.activation(a[:, 0:width], a[:, 0:width], AF.Exp, scale=-0.5), reads=[a], writes=[a])
            return a

        PRESTAT = (stage >= 3 and DBG_BR == "mafg")
        prestat = {}

        def normmod(h, l, i, u):
            bg_ensure(("fin", l, i))
            have = prestat.pop(h, False)
            for t2 in range(2):
                tt = 2 * h + t2
                ts = slice(tt * 512, (tt + 1) * 512)
                st = banks[6 + t2] if have else B()
                for kc in range(0 if have else 8):
                    sq = S()
                    sqb = sq[:].bitcast(BF16)
                    k.op("act", lambda e, kc=kc, sqb=sqb: e.activation(sqb[:, 0:512], xT[:, kc, ts], AF.Square),
                         reads=[(xT, (kc, tt))], writes=[sq])
                    mm_acc(st[:], ones_b[:], sqb[:, 0:512], kc == 0, kc == 7, [ones_b, sq], [st])
                r = rstd_from_psum(st[:], 512, 1024.0, [st])
                for kc in range(8):
                    tm = S()
                    k.op("dve", lambda e, kc=kc, tm=tm: e.scalar_tensor_tensor(
                        tm[:], xT[:, kc, ts], gs[:, l, i, kc, h:h + 1], r[:], op0=ALU.mult, op1=ALU.mult),
                        reads=[(xT, (kc, tt)), (gs, (l, i)), r], writes=[tm])
                    k.op("act", lambda e, kc=kc, tm=tm: e.activation(
                        u[:, kc, t2 * 512:(t2 + 1) * 512], tm[:], AF.Identity, bias=sh[:, l, i, kc, h:h + 1], scale=1.0),
                        reads=[tm, (sh, (l, i))], writes=[(u, (kc, t2))])

        def resid_update(bk, l, i, h, m, t2, stat=False):
            tt = 2 * h + t2
            ts = slice(tt * 512, (tt + 1) * 512)
            k.op("dve", lambda e: e.scalar_tensor_tensor(xT[:, m, ts], bk[:], gt[:, l, i, m, h:h + 1], xT[:, m, ts],
                                                         op0=ALU.mult, op1=ALU.add),
                 reads=[bk, (gt, (l, i)), (xT, (m, tt))], writes=[(xT, (m, tt))])
            if stat:
                sq = S()
                sqb = sq[:].bitcast(BF16)
                k.op("act", lambda e: e.activation(sqb[:, 0:512], xT[:, m, ts], AF.Square), reads=[(xT, (m, tt))], writes=[sq])
                mm_acc(banks[6 + t2][:], ones_b[:], sqb[:, 0:512], m == 0, m == 7, [ones_b, sq], [banks[6 + t2]])
                if m == 7 and t2 == 1:
                    prestat[h] = True

        def ffn(h, l, which, u, after=None, pre=None):
            i = 0 if which == 1 else 2
            w_in = W["w_ffn%d_in" % which]
            w_out = W["w_ffn%d_out" % which]
            with k.scope():
                hT = k.sb("hT", [128, NJ, 1024], BF16)
                wo = [k.sb("wo%d" % q, [128, NJ, 256], BF16) for q in range(2)]
                if pre is not None:
                    pre()
                normmod(h, l, i, u)
                for pc in range(11):
                    bg_step()
                    key_, parts_ = P_ffn(l, which, pc)
                    w_ = wload(parts_, key_)
                    for hc in range(2):
                        j = 2 * pc + hc
                        for t2 in range(2):
                            us = slice(t2 * 512, (t2 + 1) * 512)
                            pa = B()
                            mm(pa[:], [(w_[:, kc, hc * 128:(hc + 1) * 128], u[:, kc, us]) for kc in range(8)],
                               [w_] + [(u, (kc, t2)) for kc in range(8)], [pa])
                            pg = B()
                            mm(pg[:], [(w_[:, kc, 256 + hc * 128:256 + (hc + 1) * 128], u[:, kc, us]) for kc in range(8)],
                               [w_] + [(u, (kc, t2)) for kc in range(8)], [pg])
                            sa = S()
                            k.op("act", lambda e, pa=pa, sa=sa: e.activation(sa[:], pa[:], AF.Silu), reads=[pa], writes=[sa])
                            k.op("dve", lambda e, pg=pg, sa=sa, j=j, us=us: e.tensor_tensor(hT[:, j, us], pg[:], sa[:], ALU.mult),
                                 reads=[pg, sa], writes=[(hT, (j, t2))])
                for pc in range(4):
                    bg_step()
                    w_ = wo[pc % 2]
                    k.dma("pool", w_[:], w_out[l, :, pc * 256:(pc + 1) * 256].rearrange("(j p) n -> p j n", p=128),
                          "wo%d" % (pc % 2), writes=[w_])
                    for mc in range(2):
                        m = 2 * pc + mc
                        for t2 in range(2):
                            us = slice(t2 * 512, (t2 + 1) * 512)
                            po = B()
                            mm(po[:], [(w_[:, j, mc * 128:(mc + 1) * 128], hT[:, j, us]) for j in range(NJ)],
                               [w_] + [(hT, (j, t2)) for j in range(NJ)], [po])
                            resid_update(po, l, i, h, m, t2, stat=(PRESTAT and which == 1))
                if after is not None:
                    after()

        def proj_fm(w_, c0, u, t2, reads_extra=()):
            us = slice(t2 * 512, (t2 + 1) * 512)
            bk = B()
            mm(bk[:], [(w_[:, kc, c0:c0 + 128], u[:, kc, us]) for kc in range(8)],
               [w_] + [(u, (kc, t2)) for kc in range(8)], [bk])
            return bk

        def proj_tm(w_, c0, n, u, tc):
            t2 = tc // 4
            bk = B()
            mm(bk[:, 0:n], [(u[:, kc, tc * 128:(tc + 1) * 128], w_[:, kc, c0:c0 + n]) for kc in range(8)],
               [w_] + [(u, (kc, t2)) for kc in range(8)], [bk])
            return bk

        def feat_norm_scale(o_src, width, gcol, out_ap, out_w, srcs):
            sq = S()
            sqb = sq[:].bitcast(BF16)
            k.op("dve", lambda e: e.tensor_tensor(sqb[:, 0:width], o_src[:, 0:width], o_src[:, 0:width], ALU.mult), reads=srcs, writes=[sq])
            st = B()
            mm(st[:, 0:width], [(ones_b[:], sqb[:, 0:width])], [ones_b, sq], [st])
            r = rstd_from_psum(st[:, 0:width], width, 128.0, [st])
            dump("fn_sq", sq, sq[:], [128, 512])
            dump("fn_r", r, r[:], [128, 512])
            dump("fn_o", o_src, o_src[:], [128, 512])
            k.op("dve", lambda e: e.scalar_tensor_tensor(out_ap, o_src[:, 0:width], gcol, r[:, 0:width],
                                                         op0=ALU.mult, op1=ALU.mult),
                 reads=srcs + [r, gsubc, gmc], writes=out_w)

        def mlstm(h, l, u, moT, after=None):
            nseq, T = (4, 256) if h == 0 else (1, 1024)
            nch = T // 128
            with k.scope():
                negM = [k.sb("negM%d" % q, [4, 1024]) for q in range(2)]
                negBM = [k.sb("negBM%d" % q, [4, 1024]) for q in range(2)]
                acol = k.sb("acol", [128, 2, 8, 4])
                m1t = k.sb("m1t", [4, 2, 4])
                if h == 1:
                    m0b = k.sb("m0b", [128, 2, 4])
                    m0r = k.sb("m0r", [4, 2])
                    k.dma("sp", m0b[:].rearrange("p d g -> p (d g)"), sm_d[l].rearrange("d g -> (d g)").partition_broadcast(128),
                          "c0", writes=[m0b])
                    k.dma("sp", m0r[:], sm_d[l].rearrange("d g -> g d"), "c0", writes=[m0r], allow_slow_non_contiguous=True)
                gscope = k.scope()
                gscope.__enter__()
                gsb = k.sb("gsb", [128, 8, 16])
                G = [k.sb("G%d" % q, [4, 1024]) for q in range(4)]
                ones4 = k.sb("ones4", [4, 1024])
                k.op("dve", lambda e: e.memset(ones4[:], 1.0), writes=[ones4])
                wg = wload(P_mg(l)[1], P_mg(l)[0])
                bk = B()
                for tc in range(8):
                    mm(bk[:, tc * 16:(tc + 1) * 16], [(u[:, kc, tc * 128:(tc + 1) * 128], wg[:, kc, 0:16]) for kc in range(8)],
                       [wg] + [(u, (kc, tc // 4)) for kc in range(8)], [(bk, tc)])
                for tc in range(8):
                    k.op("dve", lambda e: e.tensor_tensor(gsb[:, tc, :], bk[:, tc * 16:(tc + 1) * 16], bmg[:, l, :], ALU.add),
                         reads=[bk, bmg], writes=[gsb])
                for ty in (1, 3):
                    gsl = gsb[:, :, 4 * ty:4 * ty + 4]
                    k.op("act", lambda e, gsl=gsl: e.activation(gsl, gsl, AF.Exp, scale=-1.0), reads=[gsb], writes=[gsb])
                    k.op("act", lambda e, gsl=gsl: e.activation(gsl, gsl, AF.Ln, bias=onec[:], scale=1.0), reads=[gsb, onec], writes=[gsb])
                    k.op("dve", lambda e, gsl=gsl: e.tensor_scalar(gsl, gsl, -1.0, None, op0=ALU.mult), reads=[gsb], writes=[gsb])
                for ty in range(4):
                    for hb in range(2):
                        bk = B()
                        for j in range(4):
                            tc = 4 * hb + j
                            transp(bk[0:4, j * 128:(j + 1) * 128], gsb[:, tc, 4 * ty:4 * ty + 4], ident[:], [gsb, ident], [(bk, j)])
                        k.op("act", lambda e, bk=bk, ty=ty, hb=hb: e.copy(G[ty][:, hb * 512:(hb + 1) * 512], bk[0:4, :]),
                             reads=[bk], writes=[G[ty]])
                for d in range(2):
                    IG, LF = G[2 * d], G[2 * d + 1]
                    for s in range(nseq):
                        def seg(tile_, lo=s * T, hi=(s + 1) * T, d=d):
                            v = tile_[:, lo:hi]
                            return v[:, ::-1] if d == 1 else v
                        tl = s * T + (T - 1 if d == 0 else 0)
                        k.op("dve", lambda e: e.tensor_tensor_scan(seg(LF), seg(ones4), seg(LF), 0.0, ALU.mult, ALU.add),
                             reads=[ones4, LF], writes=[LF])
                        k.op("dve", lambda e: e.tensor_tensor(seg(IG), seg(IG), seg(LF), ALU.subtract), reads=[IG, LF], writes=[IG])
                        init = m0r[:, d:d + 1] if h == 1 else 0.0
                        k.op("dve", lambda e: e.tensor_tensor_scan(seg(negM[d]), seg(ones4), seg(IG), init, ALU.mult, ALU.max),
                             reads=[ones4, IG] + ([m0r] if h == 1 else []), writes=[negM[d]])
                        k.op("dve", lambda e: e.tensor_tensor(seg(negBM[d]), seg(negM[d]), seg(LF), ALU.add),
                             reads=[negM[d], LF], writes=[negBM[d]])
                        if h == 0:
                            k.op("dve", lambda e: e.tensor_copy(m1t[:, d, s:s + 1], negBM[d][:, tl:tl + 1]), reads=[negBM[d]], writes=[m1t])
                    k.op("dve", lambda e: e.tensor_scalar(negM[d][:], negM[d][:], -1.0, None, op0=ALU.mult), reads=[negM[d]], writes=[negM[d]])
                    k.op("dve", lambda e: e.tensor_scalar(negBM[d][:], negBM[d][:], -1.0, None, op0=ALU.mult), reads=[negBM[d]], writes=[negBM[d]])
                    bk = B()
                    for tc in range(8):
                        transp(bk[:, tc * 4:(tc + 1) * 4], IG[:, tc * 128:(tc + 1) * 128], ident[0:4, 0:4], [IG, ident], [(bk, tc)])
                    k.op("act", lambda e, bk=bk, d=d: e.copy(acol[:, d, :, :], bk[:, 0:32].rearrange("p (c g) -> p c g", g=4)),
                         reads=[bk], writes=[acol])
                if h == 0:
                    for d in range(2):
                        store(nm_d[:, l, d, :].rearrange("s g -> g s"), m1t[:, d, :], m1t, "st_m", allow_slow_non_contiguous=True)
                gscope.__exit__(None, None, None)
                mqT = k.sb("mqT", [128, 1024], BF16)
                mkT = k.sb("mkT", [128, 1024], BF16)
                mvt = k.sb("mvt", [128, 8, 129], BF16)
                k.op("dve", lambda e: e.memset(mvt[:], 1.0), writes=[mvt])
                mkt = k.sb("mkt", [128, 8, 128], BF16)
                soT = k.sb("soT", [128, 1024])
                hs = k.sb("hs", [128, 1024])
                Wt = [k.sb("Wt%d" % q, [128, 512]) for q in range(3)]
                St = [k.sb("St%d" % q, [128, 512], BF16) for q in range(14 if h == 1 else 6)]
                Mbs = [k.sb("Mb%d" % q, [128, T]) for q in range(2)]
                Fls = [k.sb("Fl%d" % q, [128, T]) for q in range(2)]
                mbi = [0]
                si_ = [0]
                mvw = [k.sb("mvw%d" % q, [128, 129], BF16) for q in range(4)]
                cst_ = k.sb("cstage", [128, 256])
                wi_ = [0]
                if h == 1:
                    C0T = k.sb("C0T", [128, 2, 128], BF16)
                    n0c = [k.sb("n0c%d" % q, [128, 1]) for q in range(2)]
                    n0b = k.sb("n0b", [128, 2, 128], BF16)
                    c0s = k.sb("c0s", [128, 128])
                for hd in range(4):
                    wq = wload(P_mq(l, hd)[1], P_mq(l, hd)[0])
                    for t2 in range(2):
                        us = slice(t2 * 512, (t2 + 1) * 512)
                        bk = proj_fm(wq, 0, u, t2)
                        k.op("act", lambda e, bk=bk, us=us: e.copy(mqT[:, us], bk[:]), reads=[bk], writes=[(mqT, t2)])
                        bk = proj_fm(wq, 128, u, t2)
                        k.op("act", lambda e, bk=bk, us=us: e.mul(mkT[:, us], bk[:], K_SCALE), reads=[bk], writes=[(mkT, t2)])
                        bk = proj_fm(wq, 384, u, t2)
                        k.op("act", lambda e, bk=bk, us=us: e.activation(soT[:, us], bk[:], AF.Sigmoid), reads=[bk], writes=[(soT, t2)])
                    for tc in range(8):
                        if h == 0:
                            bk = proj_tm(wq, 128, 256, u, tc)
                            k.op("act", lambda e, bk=bk, tc=tc: e.mul(mkt[:, tc, :], bk[:, 0:128], K_SCALE), reads=[bk], writes=[(mkt, tc)])
                            k.op("act", lambda e, bk=bk, tc=tc: e.copy(mvt[:, tc, 0:128], bk[:, 128:256]), reads=[bk], writes=[(mvt, tc)])
                        else:
                            bk = proj_tm(wq, 256, 128, u, tc)
                            k.op("act", lambda e, bk=bk, tc=tc: e.copy(mvt[:, tc, 0:128], bk[:, 0:128]), reads=[bk], writes=[(mvt, tc)])
                    if h == 1:
                        for d in range(2):
                            k.dma("sp", c0s[:], sC_d[l, d, hd, :, :], "c0s", writes=[c0s])
                            bk = B()
                            transp(bk[:, 0:128], c0s[:], ident[:], [c0s, ident], [bk])
                            k.op("act", lambda e, bk=bk, d=d: e.copy(C0T[:, d, :], bk[:, 0:128]), reads=[bk], writes=[(C0T, d)])
                            k.dma("sp", n0c[d][:], sn_d[l, d, hd, :].rearrange("(p o) -> p o", o=1), "c0s", writes=[n0c[d]],
                                  allow_slow_non_contiguous=True)
                            k.op("dve", lambda e, d=d: e.tensor_scalar(n0b[:, d, :], ones_f[:], n0c[d][:, 0:1], None, op0=ALU.mult),
                                 reads=[ones_f, n0c[d]], writes=[(n0b, d)])
                    blocks = [(b0, min(b0 + 512, T)) for b0 in range(0, T, 512)]

                    def p1(ui, s, d, b0, b1):
                        t0 = s * T
                        tri = triu_b if d == 0 else tril_b
                        if b0 == 0:
                            mbi[0] ^= 1
                        Mb, Fl = Mbs[mbi[0]], Fls[mbi[0]]
                        if b0 == 0:
                            for (c0_, c1_) in blocks:
                                nb = c1_ - c0_
                                g0, g1 = t0 + c0_, t0 + c1_
                                eb = B()
                                mm(eb[:, 0:nb], [(sel[:, hd, :], negM[d][:, g0:g1])], [(sel, hd), negM[d]], [eb])
                                if h == 0:
                                    Mb = eb
                                else:
                                    k.op("act", lambda e: e.copy(Mb[:, c0_:c1_], eb[:, 0:nb]), reads=[eb], writes=[(Mb, c0_)])
                                fb = B()
                                mm(fb[:, 0:nb], [(sel[:, hd, :], negBM[d][:, g0:g1])], [(sel, hd), negBM[d]], [fb])
                                k.op("act", lambda e: e.activation(Fl[:, c0_:c1_], fb[:, 0:nb], AF.Exp), reads=[fb], writes=[(Fl, c0_)])
                        contrib = []
                        if h == 1:
                            contrib.append(("virt", None, b0, b1))
                        chs = list(range(nch))
                        if d == 1:
                            chs = chs[::-1]
                        for ci in chs:
                            if d == 0:
                                lo, hi = max(b0, ci * 128), b1
                            else:
                                lo, hi = b0, min(b1, ci * 128 + 128)
                            if lo < hi:
                                contrib.append(("real", ci, lo, hi))
                        ops = []
                        for idx, (kind, ci, lo, hi) in enumerate(contrib):
                            n = hi - lo
                            g0, g1 = t0 + lo, t0 + hi
                            wi_[0] = (wi_[0] + 1) % len(Wt)
                            w_t = Wt[wi_[0]]
                            si_[0] = (si_[0] + 1) % len(St)
                            s_t = St[si_[0]]
                            if kind == "virt":
                                k.op("act", lambda e: e.activation(w_t[:, 0:n], Mb[:, lo:hi], AF.Exp, bias=m0b[:, d, hd:hd + 1], scale=1.0),
                                     reads=[(Mb, b0), m0b], writes=[w_t])
                                k.op("dve", lambda e: e.tensor_tensor(s_t[:, 0:n], mqT[:, g0:g1], w_t[:, 0:n], ALU.mult),
                                     reads=[w_t, mqT], writes=[s_t])
                                lv, ld = C0T[:, d, :], n0b[:, d, :]
                                rv = [(C0T, d), (n0b, d)]
                            else:
                                c0 = t0 + ci * 128
                                tcg = c0 // 128
                                kq = B()
                                mm(kq[:, 0:n], [(mkT[:, c0:c0 + 128], mqT[:, g0:g1])], [mkT, mqT], [kq])
                                k.op("act", lambda e: e.activation(w_t[:, 0:n], Mb[:, lo:hi], AF.Exp, bias=acol[:, d, tcg, hd:hd + 1], scale=1.0),
                                     reads=[Mb if h == 0 else (Mb, b0), acol], writes=[w_t])
                                k.op("dve", lambda e: e.tensor_tensor(s_t[:, 0:n], kq[:, 0:n], w_t[:, 0:n], ALU.mult),
                                     reads=[kq, w_t], writes=[s_t])
                                if d == 0 and lo == ci * 128:
                                    k.op("dve", lambda e: e.tensor_tensor(s_t[:, 0:128], s_t[:, 0:128], tri[:], ALU.mult),
                                         reads=[s_t, tri], writes=[s_t])
                                if d == 1 and hi == ci * 128 + 128:
                                    k.op("dve", lambda e: e.tensor_tensor(s_t[:, n - 128:n], s_t[:, n - 128:n], tri[:], ALU.mult),
                                         reads=[s_t, tri], writes=[s_t])
                                lv, ld = mvt[:, tcg, 0:128], ones_b[:]
                                rv = [(mvt, tcg), ones_b]
                                if h == 0 and ((d == 0 and hi == T) or (d == 1 and lo == 0)):
                                    colw = (n - 1) if d == 0 else 0
                                    mw = mvw[(ui % 2) * 2 + ci % 2]
                                    k.op("dve", lambda e: e.tensor_scalar(mw[:, 0:129], mvt[:, tcg, 0:129], w_t[:, colw:colw + 1], None, op0=ALU.mult),
                                         reads=[(mvt, tcg), w_t], writes=[mw])
                            ops.append((lv, ld, rv, s_t, lo - b0, n))
                        return (ui, s, d, b0, b1, ops, Fl)

                    def p2(ctx):
                        ui, s, d, b0, b1, ops, Fl = ctx
                        t0 = s * T
                        nct = len(ops)
                        ACC0, ACC1 = ACCP[ui % 2]
                        for idx, (lv, ld, rv, s_t, o0, n) in enumerate(ops):
                            mm_acc(ACC0[:, o0:o0 + n], lv, s_t[:, 0:n], idx == 0, idx == nct - 1, rv + [s_t], [ACC0])
                        for idx, (lv, ld, rv, s_t, o0, n) in enumerate(ops):
                            mm_acc(ACC1[:, o0:o0 + n], ld, s_t[:, 0:n], idx == 0, idx == nct - 1, rv + [s_t], [ACC1])
                        nb = b1 - b0
                        g0, g1 = t0 + b0, t0 + b1
                        da = S()
                        k.op("act", lambda e: e.activation(da[:, 0:nb], ACC1[:, 0:nb], AF.Abs), reads=[ACC1], writes=[da])
                        k.op("dve", lambda e: e.tensor_tensor(da[:, 0:nb], da[:, 0:nb], Fl[:, b0:b1], ALU.max), reads=[da, (Fl, b0)], writes=[da])
                        k.op("act", lambda e: e.activation(da[:, 0:nb], da[:, 0:nb], AF.Ln), reads=[da], writes=[da])
                        k.op("act", lambda e: e.activation(da[:, 0:nb], da[:, 0:nb], AF.Exp, scale=-1.0), reads=[da], writes=[da])
                        if d == 0:
                            k.op("dve", lambda e: e.tensor_tensor(hs[:, g0:g1], ACC0[:, 0:nb], da[:, 0:nb], ALU.mult),
                                 reads=[ACC0, da], writes=[(hs, g0)])
                        else:
                            tm = S()
                            k.op("dve", lambda e: e.tensor_tensor(tm[:, 0:nb], ACC0[:, 0:nb], da[:, 0:nb], ALU.mult),
                                 reads=[ACC0, da], writes=[tm])
                            k.op("dve", lambda e: e.tensor_tensor(hs[:, g0:g1], hs[:, g0:g1], tm[:, 0:nb], ALU.add),
                                 reads=[(hs, g0), tm], writes=[(hs, g0)])
                        if h == 0:
                            cb = B()
                            for ci in range(nch):
                                tcg = (t0 + ci * 128) // 128
                                mw = mvw[(ui % 2) * 2 + ci % 2]
                                mm_acc(cb[:, 0:128], mw[:, 0:128], mkt[:, tcg, :], ci == 0, ci == nch - 1, [mw, (mkt, tcg)], [cb])
                            for ci in range(nch):
                                tcg = (t0 + ci * 128) // 128
                                mw = mvw[(ui % 2) * 2 + ci % 2]
                                mm_acc(cb[0:1, 128:256], mw[:, 128:129], mkt[:, tcg, :], ci == 0, ci == nch - 1, [mw, (mkt, tcg)], [cb])
                            k.op("act", lambda e: e.copy(cst_[:, 0:256], cb[:, 0:256]), reads=[cb], writes=[cst_])
                            store(nC_d[s, l, d, hd, :, :], cst_[:, 0:128], cst_, "st_c")
                            store(nn_d[s, l, d, hd, :].rearrange("(o k) -> o k", o=1), cst_[0:1, 128:256], cst_, "st_c")
                        if d == 1 and b1 == T:
                            pend_n.append(s)

                    def head_norm(s):
                        t0 = s * T
                        for (c0_, c1_) in blocks:
                            nb2 = c1_ - c0_
                            q0_, q1_ = t0 + c0_, t0 + c1_
                            ho = S()
                            k.op("act", lambda e: e.copy(ho[:, 0:nb2], hs[:, q0_:q1_]), reads=[(hs, q0_)], writes=[ho])
                            feat_norm_scale(ho, nb2, gmc[:, l:l + 1], ho[:, 0:nb2], [ho], [ho])
                            k.op("dve", lambda e: e.tensor_tensor(moT[:, hd, q0_:q1_], ho[:, 0:nb2], soT[:, q0_:q1_], ALU.mult),
                                 reads=[ho, soT], writes=[(moT, hd)])

                    pend_n = []

                    def run_p2(ctx):
                        nb4 = len(pend_n)
                        p2(ctx)
                        if nb4 > 0:
                            head_norm(pend_n.pop(0))

                    units = [(s, d, b0, b1) for s in range(nseq) for d in range(2) for (b0, b1) in blocks]
                    prev = None
                    nrot[0] = 4
                    for ui, (s, d, b0, b1) in enumerate(units):
                        ncur = (1 if h == 1 else 0) + sum(
                            1 for ci in range(nch)
                            if (max(b0, ci * 128) < b1 if d == 0 else b0 < min(b1, ci * 128 + 128)))
                        if prev is not None and len(prev[5]) + ncur > len(St):
                            run_p2(prev)
                            prev = None
                        cur = p1(ui, s, d, b0, b1)
                        if prev is not None:
                            run_p2(prev)
                        prev = cur
                    if prev is not None:
                        run_p2(prev)
                    while pend_n:
                        head_norm(pend_n.pop(0))
                    nrot[0] = 6
                if after is not None:
                    after()

        def attention(h, l, u, aoT, after=None):
            nseq, T = (4, 256) if h == 0 else (1, 1024)
            NK = 256 if h == 0 else 1280
            nkc = NK // 128
            with k.scope():
                qT = k.sb("qT", [128, 4, 1024], BF16)
                kT = k.sb("kT", [128, 4, 1280 if h == 1 else 1024], BF16)
                vt = k.sb("vt", [128, 10 if h == 1 else 8, 512], BF16)
                PTs = [k.sb("PT%d" % q, [128, 10, 512] if h == 1 else [128, 2, 256], BF16) for q in range(2)]
                ocs = [k.sb("oc%d" % q, [128, 512 if h == 1 else 256]) for q in range(2)]
                if h == 0:
                    stg = [k.sb("stg%d" % q, [128, 512]) for q in range(2)]
                rscope = k.scope()
                rscope.__enter__()
                if h == 1:
                    cosT = k.sb("cosT", [128, 1024])
                    sinT = k.sb("sinT", [128, 1024])
                    permf = k.sb("permf", [128, 128])
                    kcs = k.sb("kcs", [128, 2, 4, 128])
                    k.dma("sp", cosT[:], C["cosT"][:, :], "c0", writes=[cosT])
                    k.dma("sp", sinT[:], C["sinT"][:, :], "c0", writes=[sinT])
                    k.dma("sp", permf[:], C["perm"][:, :], "c0", writes=[permf])
                    for c in range(2):
                        k.dma("sp", kcs[:, c, :, :], ck_d[l, :, c * 128:(c + 1) * 128, :].rearrange("g p d -> p g d"), "c0", writes=[kcs])
                    for hd in range(4):
                        bk = B()
                        for c in range(2):
                            transp(bk[:, c * 128:(c + 1) * 128], kcs[:, c, hd, :], ident[:], [kcs, ident], [(bk, c)])
                        k.op("act", lambda e, bk=bk, hd=hd: e.copy(kT[:, hd, 0:256], bk[:, 0:256]), reads=[bk], writes=[(kT, hd)])
                    for c in range(2):
                        k.dma("pool", vt[:, c, :].rearrange("p (g d) -> p g d", g=4), cv_d[l, :, c * 128:(c + 1) * 128, :].rearrange("g p d -> p g d"),
                              "c1", writes=[(vt, 0), (vt, 1)])
                koff = 256 if h == 1 else 0
                if DBG_AT < -1:
                    return

                def rope_or_copy(bk, dst_ap, us, dst_w):
                    if h == 0:
                        k.op("act", lambda e: e.copy(dst_ap, bk[:]), reads=[bk], writes=dst_w)
                        return
                    q32 = S()
                    k.op("act", lambda e: e.copy(q32[:], bk[:]), reads=[bk], writes=[q32])
                    pb = B()
                    mm(pb[:], [(permf[:], q32[:])], [permf, q32], [pb])
                    t1 = S()
                    k.op("dve", lambda e: e.tensor_tensor(t1[:], q32[:], cosT[:, us], ALU.mult), reads=[q32, cosT], writes=[t1])
                    t2_ = S()
                    k.op("dve", lambda e: e.tensor_tensor(t2_[:], pb[:], sinT[:, us], ALU.mult), reads=[pb, sinT], writes=[t2_])
                    k.op("dve", lambda e: e.tensor_tensor(dst_ap, t1[:], t2_[:], ALU.add), reads=[t1, t2_], writes=dst_w)

                wq = wload(P_at(l)[1], P_at(l)[0])
                for hd in range(4):
                    for t2 in range(2):
                        us = slice(t2 * 512, (t2 + 1) * 512)
                        bk = proj_fm(wq, hd * 128, u, t2)
                        rope_or_copy(bk, qT[:, hd, us], us, [(qT, (hd, t2))])
                wk = wload([(slice(0, 512), W["w_in"][l, :, 512:1024])])
                for hd in range(4):
                    for t2 in range(2):
                        us = slice(t2 * 512, (t2 + 1) * 512)
                        bk = proj_fm(wk, hd * 128, u, t2)
                        rope_or_copy(bk, kT[:, hd, koff + t2 * 512:koff + (t2 + 1) * 512], us, [(kT, hd)])
                if DBG_AT < 0:
                    return
                if h == 0 and (DBG_TM & 1):
                    for tc in range(8):
                        bk = proj_tm(wk, 0, 512, u, tc)
                        sg_ = stg[tc % 2]
                        k.op("act", lambda e, bk=bk, sg_=sg_: e.copy(sg_[:], bk[:]), reads=[bk], writes=[sg_])
                        s, c = tc // 2, tc % 2
                        store(nk_d[s, l, :, c * 128:(c + 1) * 128, :].rearrange("g t d -> t g d"),
                              sg_[:].rearrange("p (g d) -> p g d", g=4), sg_, "st_k%d" % (tc % 2))
                wv = wload([(slice(0, 512), W["w_in"][l, :, 1024:1536])])
                for tc in range(8 if (DBG_TM & 2) else 0):
                    bk = proj_tm(wv, 0, 512, u, tc)
                    vc = tc + (2 if h == 1 else 0)
                    if h == 1:
                        k.op("act", lambda e, bk=bk, vc=vc: e.copy(vt[:, vc, :], bk[:]), reads=[bk], writes=[(vt, vc)])
                    else:
                        sg_ = stg[tc % 2]
                        k.op("act", lambda e, bk=bk, sg_=sg_: e.copy(sg_[:], bk[:]), reads=[bk], writes=[sg_])
                        k.op("dve", lambda e, sg_=sg_, vc=vc: e.tensor_copy(vt[:, vc, :], sg_[:]), reads=[sg_], writes=[(vt, vc)])
                        s, c = tc // 2, tc % 2
                        store(nv_d[s, l, :, c * 128:(c + 1) * 128, :].rearrange("g t d -> t g d"),
                              sg_[:].rearrange("p (g d) -> p g d", g=4), sg_, "st_k%d" % (tc % 2))
                rscope.__exit__(None, None, None)
                qblocks = [(s * 256, 256, s) for s in range(4)] if h == 0 else [(0, 512, 0), (512, 512, 0)]
                units = [(q0, nq, s, hd, m) for (q0, nq, s) in qblocks for hd in range(4) for m in range(2)]
                om = {}
                pend_c = []

                def stage_a(ui):
                    q0, nq, s, hd, m = units[ui]
                    PT = PTs[ui % 2]
                    ps_ = slice(64 * m, 64 * m + 64)
                    for kc in range(nkc):
                        k0 = (s * 256 if h == 0 else 0) + kc * 128
                        sb_ = B()
                        mm(sb_[:, 0:nq], [(kT[ps_, hd, k0:k0 + 128], qT[ps_, hd, q0:q0 + nq])],
                           [(kT, hd), (qT, (hd, q0 // 512))], [sb_])
                        k.op("act", lambda e: e.activation(PT[:, kc, 0:nq], sb_[:, 0:nq], AF.Exp, scale=0.125),
                             reads=[sb_], writes=[(PT, kc)])

                def stage_b(ui):
                    q0, nq, s, hd, m = units[ui]
                    PT = PTs[ui % 2]
                    vbase = (s * 2 if h == 0 else 0)
                    ob, db = ACCP[ui % 2]
                    mm(ob[:, 0:nq], [(vt[:, vbase + kc, hd * 128:(hd + 1) * 128], PT[:, kc, 0:nq]) for kc in range(nkc)],
                       [(vt, vbase + kc) for kc in range(nkc)] + [(PT, kc) for kc in range(nkc)], [ob])
                    mm(db[:, 0:nq], [(ones_b[:], PT[:, kc, 0:nq]) for kc in range(nkc)],
                       [ones_b] + [(PT, kc) for kc in range(nkc)], [db])
                    rc = S()
                    k.op("act", lambda e: e.activation(rc[:, 0:nq], db[:, 0:nq], AF.Ln), reads=[db], writes=[rc])
                    k.op("act", lambda e: e.activation(rc[:, 0:nq], rc[:, 0:nq], AF.Exp, scale=-1.0), reads=[rc], writes=[rc])
                    o_ = S()
                    k.op("dve", lambda e: e.tensor_tensor(o_[:, 0:nq], ob[:, 0:nq], rc[:, 0:nq], ALU.mult),
                         reads=[ob, rc], writes=[o_])
                    om[m] = o_
                    if m == 1:
                        oc = ocs[(ui // 2) % 2]
                        k.op("dve", lambda e: e.scalar_tensor_tensor(oc[:, 0:nq], om[1][:, 0:nq], nlam[:, l:l + 1], om[0][:, 0:nq],
                                                                     op0=ALU.mult, op1=ALU.add),
                             reads=[om[0], om[1], nlam], writes=[oc])
                        pend_c.append((oc, nq, hd, q0))

                def stage_c():
                    oc, nq, hd, q0 = pend_c.pop(0)
                    feat_norm_scale(oc, nq, gsubc[:, l:l + 1], aoT[:, hd, q0:q0 + nq], [(aoT, (hd, q0))], [oc])

                nrot[0] = 4
                stage_a(0)
                for ui in range(len(units)):
                    if ui + 1 < len(units):
                        stage_a(ui + 1)
                    if len(pend_c) > 0 and ui % 2 == 1:
                        stage_c()
                    stage_b(ui)
                while pend_c:
                    stage_c()
                nrot[0] = 6
                if after is not None:
                    after()

        def fnet(h, l, u, foT, after=None):
            nseq, T = (4, 256) if h == 0 else (1, 1024)
            nch = T // 128
            with k.scope():
                zfT = k.sb("zfT", [128, 4, 1024], BF16)
                Y = k.sb("Y", [128, 8, 4, 256], BF16)
                dftd = k.sb("dftd", [128, 256], BF16)
                dc = k.sb("dc", [128, nch, T], BF16)
                dns = k.sb("dns", [128, nch, T], BF16)
                wf = wload(P_fn(l)[1], P_fn(l)[0])
                k.dma("pool", dftd[:], C["dftd"][:, :], "c1", writes=[dftd])
                k.dma("pool", dc[:], C["dftc%d" % T].rearrange("(c p) t -> p c t", p=128), "c1", writes=[dc])
                k.dma("pool", dns[:], C["dftns%d" % T].rearrange("(c p) t -> p c t", p=128), "c1", writes=[dns])
                for g in range(4):
                    for t2 in range(2):
                        us = slice(t2 * 512, (t2 + 1) * 512)
                        bk = proj_fm(wf, g * 128, u, t2)
                        k.op("act", lambda e, bk=bk, g=g, us=us: e.copy(zfT[:, g, us], bk[:]), reads=[bk], writes=[(zfT, (g, t2))])
                for tc in range(8):
                    for gp in range(2):
                        bk = B()
                        for gg in range(2):
                            g = 2 * gp + gg
                            mm(bk[:, gg * 256:(gg + 1) * 256], [(zfT[:, g, tc * 128:(tc + 1) * 128], dftd[:])],
                               [(zfT, (g, tc // 4)), dftd], [(bk, gg)])
                        k.op("act", lambda e, bk=bk, tc=tc, gp=gp: e.copy(Y[:, tc, 2 * gp:2 * gp + 2, :], bk[:].rearrange("p (g n) -> p g n", g=2)),
                             reads=[bk], writes=[(Y, (tc, gp))])
                sc_ = 1.0 / math.sqrt(T * 128.0)
                for s in range(nseq):
                    for g in range(4):
                        for b0 in range(0, T, 512):
                            nb = min(512, T - b0)
                            pairs = []
                            rd = [dc, dns]
                            for c in range(nch):
                                tcg = s * nch + c
                                pairs.append((Y[:, tcg, g, 0:128], dc[:, c, b0:b0 + nb]))
                                pairs.append((Y[:, tcg, g, 128:256], dns[:, c, b0:b0 + nb]))
                                rd.append((Y, (tcg, g // 2)))
                            bk = B()
                            mm(bk[:, 0:nb], pairs, rd, [bk])
                            k.op("act", lambda e, bk=bk, g=g, s=s, b0=b0, nb=nb: e.mul(foT[:, g, s * T + b0:s * T + b0 + nb], bk[:, 0:nb], sc_),
                                 reads=[bk], writes=[(foT, (g, s * T + b0))])
                if after is not None:
                    after()

        def merge(h, l, u, brs, after=None):
            with k.scope():
                mg = k.sb("mg", [128, 8, 1024], BF16)
                wbr = [k.sb("wbr%d" % q, [128, 4, 3, 128], BF16) for q in range(2)]
                wnames = ["w_br_attn", "w_br_four", "w_br_mlstm"]
                for m in range(8):
                    wg = wload(P_me(l, m)[1], P_me(l, m)[0])
                    wb = wbr[m % 2]
                    for br in range(3):
                        k.dma("pool", wb[:, :, br, :], W[wnames[br]][l, :, m * 128:(m + 1) * 128].rearrange("(j p) n -> p j n", p=128),
                              "wbr%d" % (m % 2), writes=[wb])
                    for t2 in range(2):
                        us = slice(t2 * 512, (t2 + 1) * 512)
                        acc = S()
                        for br in range(3):
                            gb = B()
                            mm(gb[:], [(wg[:, kc, br * 128:(br + 1) * 128], u[:, kc, us]) for kc in range(8)],
                               [wg] + [(u, (kc, t2)) for kc in range(8)], [gb])
                            sg_ = S()
                            k.op("act", lambda e, gb=gb, sg_=sg_: e.activation(sg_[:], gb[:], AF.Sigmoid), reads=[gb], writes=[sg_])
                            bb = B()
                            mm(bb[:], [(wb[:, j, br, :], brs[br][:, j, us]) for j in range(4)], [wb, brs[br]], [bb])
                            if br == 0:
                                k.op("dve", lambda e, bb=bb, sg_=sg_: e.tensor_tensor(acc[:], bb[:], sg_[:], ALU.mult), reads=[bb, sg_], writes=[acc])
                            else:
                                tm = S()
                                k.op("dve", lambda e, bb=bb, sg_=sg_, tm=tm: e.tensor_tensor(tm[:], bb[:], sg_[:], ALU.mult), reads=[bb, sg_], writes=[tm])
                                if br == 1:
                                    k.op("dve", lambda e, tm=tm: e.tensor_tensor(acc[:], acc[:], tm[:], ALU.add), reads=[acc, tm], writes=[acc])
                                else:
                                    k.op("dve", lambda e, tm=tm: e.tensor_tensor(mg[:, m, us], acc[:], tm[:], ALU.add), reads=[acc, tm], writes=[(mg, (m, t2))])
                for pc in range(2):
                    wo_ = wload([(slice(0, 512), W["w_out"][l, :, pc * 512:(pc + 1) * 512])])
                    for mc in range(4):
                        m = 4 * pc + mc
                        for t2 in range(2):
                            us = slice(t2 * 512, (t2 + 1) * 512)
                            po = B()
                            mm(po[:], [(wo_[:, kc, mc * 128:(mc + 1) * 128], mg[:, kc, us]) for kc in range(8)],
                               [wo_] + [(mg, (kc, t2)) for kc in range(8)], [po])
                            resid_update(po, l, 1, h, m, t2, stat=PRESTAT)
                if after is not None:
                    after()

        def final_chunk(c, gfb, xo):
            dst = yp_d if c < 8 else ys_d
            r0 = (c % 8) * 128
            xo_ = xo[c % 2]
            for g in range(2):
                bk = B()
                for j in range(4):
                    kc = 4 * g + j
                    transp(bk[:, j * 128:(j + 1) * 128], xT[:, kc, c * 128:(c + 1) * 128], ident[:], [(xT, (kc, c // 4)), ident], [(bk, j)])
                k.op("act", lambda e: e.copy(xo_[:, g * 512:(g + 1) * 512], bk[:]), reads=[bk], writes=[(xo_, g)])
            sq = S(); sq2 = S(); ss = S()
            k.op("dve", lambda e: e.tensor_tensor(sq[:], xo_[:, 0:512], xo_[:, 0:512], ALU.mult), reads=[xo_], writes=[sq])
            k.op("dve", lambda e: e.tensor_tensor(sq2[:], xo_[:, 512:1024], xo_[:, 512:1024], ALU.mult), reads=[xo_], writes=[sq2])
            k.op("dve", lambda e: e.tensor_tensor(sq[:], sq[:], sq2[:], ALU.add), reads=[sq, sq2], writes=[sq])
            k.op("dve", lambda e: e.reduce_sum(ss[:, 0:1], sq[:], AX.X), reads=[sq], writes=[ss])
            k.op("act", lambda e: e.activation(ss[:, 1:2], ss[:, 0:1], AF.Sqrt, bias=epsc[:], scale=1.0 / 1024.0), reads=[ss, epsc], writes=[ss])
            k.op("dve", lambda e: e.reciprocal(ss[:, 2:3], ss[:, 1:2]), reads=[ss], writes=[ss])
            k.op("dve", lambda e: e.scalar_tensor_tensor(xo_[:], xo_[:], ss[:, 2:3], gfb[:], op0=ALU.mult, op1=ALU.mult),
                 reads=[xo_, ss, gfb], writes=[xo_])
            store(dst[r0:r0 + 128, :], xo_[:], xo_, "st_y")

        def final_alloc():
            gfb = k.sb("gfb", [128, 1024])
            xo = [k.sb("xo%d" % i, [128, 1024]) for i in range(2)]
            k.dma("sp", gfb[:], W["g_final"].partition_broadcast(128), "c0", writes=[gfb])
            return gfb, xo

        hooks = (stage >= 3 and DBG_BR == "mafg")
        early_final = [False]
        seq = [(l, h) for l in range(nlayers) for h in range(2)]
        for si, (l, h) in enumerate(seq):
            nxt = seq[si + 1] if si + 1 < len(seq) else None
            if stage >= 1:
                def a1():
                    bg_ensure(("fin", l, 1))
                    wprefetch(*P_mg(l))
                    wprefetch(*P_mq(l, 0))
                with k.scope():
                    u = k.sb("u", [128, 8, 1024], BF16)
                    ffn(h, l, 1, u, after=a1 if hooks else None)
            if stage >= 2:
                def a5():
                    bg_ensure(("fin", l, 2))
                    wprefetch(*P_ffn(l, 2, 0))
                with k.scope():
                    u = k.sb("u", [128, 8, 1024], BF16)
                    normmod(h, l, 1, u)
                    moT = k.sb("moT", [128, 4, 1024], BF16)
                    if "m" in DBG_BR:
                        mlstm(h, l, u, moT, after=(lambda: wprefetch(*P_at(l))) if hooks else None)
                    else:
                        k.op("dve", lambda e: e.memset(moT[:], 0.0), writes=[moT])
                    aoT = k.sb("aoT", [128, 4, 1024], BF16)
                    if "a" in DBG_BR:
                        attention(h, l, u, aoT, after=(lambda: wprefetch(*P_fn(l))) if hooks else None)
                    else:
                        k.op("dve", lambda e: e.memset(aoT[:], 0.0), writes=[aoT])
                    foT = k.sb("foT", [128, 4, 1024], BF16)
                    if "f" in DBG_BR:
                        fnet(h, l, u, foT, after=(lambda: wprefetch(*P_me(l, 0))) if hooks else None)
                    else:
                        k.op("dve", lambda e: e.memset(foT[:], 0.0), writes=[foT])
                    dump("aoT%d" % h, aoT, aoT[:], [128, 4, 1024])
                    dump("foT%d" % h, foT, foT[:], [128, 4, 1024])
                    dump("moT%d" % h, moT, moT[:], [128, 4, 1024])
                    if "g" in DBG_BR:
                        merge(h, l, u, [aoT, foT, moT], after=a5 if hooks else None)
            if stage >= 3:
                def a6():
                    if nxt is not None:
                        bg_ensure(("fin", nxt[0], 0))
                        wprefetch(*P_ffn(nxt[0], 1, 0))

                def pre_final():
                    gfb, xo = final_alloc()
                    for c in range(8):
                        bgq.append((("final", c), (lambda c=c: final_chunk(c, gfb, xo))))
                    early_final[0] = True
                with k.scope():
                    u = k.sb("u", [128, 8, 1024], BF16)
                    last = hooks and nxt is None and h == 1
                    ffn(h, l, 2, u, after=a6 if hooks else None, pre=pre_final if last else None)
                    if last:
                        while bgq:
                            bg_step()

        with k.scope():
            gfb, xo = final_alloc()
            for c in range(8 if early_final[0] else 0, 16):
                final_chunk(c, gfb, xo)
            deps = {}
            for st in outs.st.values():
                s_, v_ = st[0]
                if deps.get(id(s_), (None, -1))[1] < v_:
                    deps[id(s_)] = (s_, v_)
            k._emit_waits("sp", deps)
        print("program: %d instructions, %d waits, %d dma sems, sbuf high-water %d / %d" % (k.n_ins, k.n_wait, len(k.all_dsems), k.hw, nc.sbuf_top))
    return nc, cst


_CACHE = {}


def _run(inputs, stage=99, nlayers=DEPTH, ncores=NCORES):
    key = (stage, nlayers)
    if key not in _CACHE:
        _CACHE[key] = build_program(stage, nlayers)
    nc, cst = _CACHE[key]
    f = lambda a: np.ascontiguousarray(np.asarray(a, dtype=np.float32))
    wts = {n: f(inputs[n]) for n in WEIGHT_NAMES}
    csts = {"c_" + n: f(v) for n, v in cst.items()}
    xp = f(inputs["x_prompt"]); xs = f(inputs["x_sample"])
    ck = f(inputs["cache_k"]); cv = f(inputs["cache_v"])
    sC = f(inputs["state_C"]); sn = f(inputs["state_n"]); sm = f(inputs["state_m"])
    cc = f(inputs["c"]); cctx = f(inputs["c_ctx"])
    in_maps = []
    for c in range(ncores):
        b = c // 2
        m = {"xp": xp[4 * c:4 * c + 4].reshape(1024, 1024), "xs": xs[b], "ck": ck[b], "cv": cv[b], "sC": sC[b],
             "sn": sn[b], "sm": sm[b], "cvec": np.stack([cctx, cc[b]], axis=0)}
        m.update(wts)
        m.update(csts)
        in_maps.append(m)
    res = run_bass_kernel_spmd(nc, in_maps, core_ids=list(range(ncores)))
    return res.results


def kernel(**inputs):
    r = _run(inputs)
    y_prompt = np.concatenate([r[c]["yp"].reshape(4, 256, 1024) for c in range(NCORES)], axis=0)
    y_sample = np.stack([r[2 * b]["ys"] for b in range(4)], axis=0)
    nk = np.concatenate([r[c]["nk"] for c in range(NCORES)], axis=0)
    nv = np.concatenate([r[c]["nv"] for c in range(NCORES)], axis=0)
    nC = np.concatenate([r[c]["nC"] for c in range(NCORES)], axis=0)
    nn = np.concatenate([r[c]["nn"] for c in range(NCORES)], axis=0)
    nm = np.concatenate([r[c]["nm"] for c in range(NCORES)], axis=0)
    return (y_prompt.astype(np.float32), y_sample.astype(np.float32), nk.astype(np.float32), nv.astype(np.float32),
            nC.astype(np.float32), nn.astype(np.float32), nm.astype(np.float32))
```

```python
import math
from contextlib import ExitStack, contextmanager
import numpy as np
import concourse.bass as bass
import concourse.mybir as mybir
from concourse.bass_utils import run_bass_kernel_spmd

F32 = mybir.dt.float32
BF16 = mybir.dt.bfloat16
AF = mybir.ActivationFunctionType
ALU = mybir.AluOpType
AX = mybir.AxisListType

D = 1024
DEPTH = 2
NCORES = 8
DFF = 2816
NJ = DFF // 128
P_IN = 4112
EPS = 1e-6
K_SCALE = 128 ** -0.5
import os
DBG_BR = os.environ.get("DBG_BR", "mafg")
DBG_AT = int(os.environ.get("DBG_AT", "9"))
DBG_TM = int(os.environ.get("DBG_TM", "7"))


class Buf:
    def __init__(self, name, t):
        self.name = name
        self.t = t
        self.st = {}

    def __getitem__(self, idx):
        return self.t[idx]


class K:
    ENG = ("pe", "act", "dve", "pool", "sp")

    def __init__(self, nc, root):
        self.nc = nc
        self.root = root
        self.es = root
        self.eng = {"pe": nc.tensor, "act": nc.scalar, "dve": nc.vector, "pool": nc.gpsimd, "sp": nc.sync}
        self.sem = {}
        self.tick = {}
        self.seen = {e: {} for e in self.ENG}
        for e in self.ENG:
            self.sem[e] = root.enter_context(nc.semaphore("tk_" + e))
            self.tick[e] = 0
        self.dsem = {}
        self.free_sems = {}
        self.all_dsems = []
        self.scope_bufs = [[]]
        self.ctr = 0
        self.n_ins = 0
        self.n_wait = 0

    def sb(self, name, shape, dtype=F32):
        self.ctr += 1
        t = self.es.enter_context(self.nc.sbuf_tensor("%s_%d" % (name, self.ctr), list(shape), dtype))
        b = Buf(name, t)
        self.scope_bufs[-1].append(b)
        self.hw = max(getattr(self, "hw", 0), self.nc.sbuf_base)
        return b

    def ps(self, name, shape, dtype=F32):
        t = self.es.enter_context(self.nc.psum_tensor(name, list(shape), dtype))
        return Buf(name, t)

    @contextmanager
    def scope(self):
        old = self.es
        with ExitStack() as sub:
            self.es = sub
            self.scope_bufs.append([])
            yield
            self.barrier()
            for b in self.scope_bufs.pop():
                for qt in ("sw", "hw"):
                    ent = self.dsem.pop((id(b), qt), None)
                    if ent is not None:
                        self.free_sems.setdefault(qt, []).append(ent)
            self.es = old

    @staticmethod
    def _norm(x):
        return (x, None) if isinstance(x, Buf) else x

    @staticmethod
    def _states(buf, key):
        if key is None:
            return list(buf.st.values())
        out = []
        if None in buf.st:
            out.append(buf.st[None])
        if key in buf.st:
            out.append(buf.st[key])
        return out

    def _deps(self, reads, writes):
        deps = {}

        def add(tok):
            if tok is None:
                return
            s, v = tok
            if deps.get(id(s), (None, -1))[1] < v:
                deps[id(s)] = (s, v)

        for r in reads:
            b, key = self._norm(r)
            for st in self._states(b, key):
                add(st[0])
        for w in writes:
            b, key = self._norm(w)
            for st in self._states(b, key):
                add(st[0])
                for tok in st[1].values():
                    add(tok)
        return deps

    def _record(self, reads, writes, tok):
        for r in reads:
            b, key = self._norm(r)
            st = b.st.setdefault(key, [None, {}])
            st[1][id(tok[0])] = tok
        for w in writes:
            b, key = self._norm(w)
            if key is None:
                b.st = {None: [tok, {}]}
            else:
                b.st[key] = [tok, {}]

    def _emit_waits(self, e, deps):
        eng = self.eng[e]
        seen = self.seen[e]
        for sid, (s, v) in deps.items():
            if e == "pe" and s is self.sem["pe"]:
                continue
            if seen.get(sid, -1) >= v:
                continue
            eng.wait_ge(s, v)
            seen[sid] = v
            self.n_wait += 1

    def op(self, e, fn, reads=(), writes=()):
        return self.group(e, [fn], reads, writes)

    def group(self, e, fns, reads=(), writes=()):
        deps = self._deps(reads, writes)
        self._emit_waits(e, deps)
        ins = None
        for fn in fns:
            ins = fn(self.eng[e])
        self.tick[e] += 1
        ins.then_inc(self.sem[e], 1)
        self._record(reads, writes, (self.sem[e], self.tick[e]))
        self.n_ins += len(fns)
        return ins

    def dma(self, q, out, in_, sem, reads=(), writes=(), **kw):
        if isinstance(sem, str):
            sem = self._norm(list(writes)[0])[0]
        deps = self._deps(reads, writes)
        self._emit_waits(q, deps)
        qt = "sw" if q == "pool" else "hw"
        skey = (id(sem), qt)
        if skey not in self.dsem:
            fl = self.free_sems.setdefault(qt, [])
            if fl:
                self.dsem[skey] = fl.pop()
            else:
                ent = [self.root.enter_context(self.nc.semaphore("d%s_%d" % (qt, len(self.all_dsems)))), 0,
                       id(sem) in getattr(self, "nobar", ())]
                self.all_dsems.append(ent)
                self.dsem[skey] = ent
        ent = self.dsem[skey]
        ins = self.eng[q].dma_start(out=out, in_=in_, **kw)
        ent[1] += 16
        ins.then_inc(ent[0], 16)
        self._record(reads, writes, (ent[0], ent[1]))
        self.n_ins += 1
        return ins

    def barrier(self):
        toks = [(self.sem[e], self.tick[e]) for e in self.ENG if self.tick[e] > 0]
        toks += [(ent[0], ent[1]) for ent in self.all_dsems if ent[1] > 0 and not (len(ent) > 2 and ent[2])]
        for e in self.ENG:
            self._emit_waits(e, {id(s): (s, v) for (s, v) in toks if s is not self.sem[e]})


def _consts():
    c = {}
    c["ident"] = np.eye(128, dtype=np.float32)
    p = np.arange(128)
    first = (p % 64) < 32
    partner = np.where(first, p + 32, p - 32)
    perm = np.zeros((128, 128), np.float32)
    perm[partner, p] = 1.0
    c["perm"] = perm
    t = np.arange(1024)
    row = (t // 64).astype(np.float32)
    col = (t % 64).astype(np.float32)
    inv = (np.float32(10000.0) ** (-np.arange(16, dtype=np.float32) / np.float32(16))).astype(np.float32)
    ang = np.concatenate([row[:, None] * inv, col[:, None] * inv], axis=-1).astype(np.float32)
    cosv = np.cos(ang).astype(np.float32)
    sinv = np.sin(ang).astype(np.float32)
    i = p % 32
    c["cosT"] = np.ascontiguousarray(cosv[:, i].T)
    c["sinT"] = np.ascontiguousarray((sinv[:, i] * np.where(first, -1.0, 1.0)[None, :]).T.astype(np.float32))
    s = np.arange(128)
    c["triu"] = (s[:, None] <= s[None, :]).astype(np.float32)
    c["tril"] = (s[:, None] >= s[None, :]).astype(np.float32)
    c["mcol4"] = np.eye(4, dtype=np.float32)
    for T in (256, 1024):
        tt = np.arange(T, dtype=np.float64)
        a = 2.0 * np.pi * ((tt[:, None] * tt[None, :]) % T) / T
        c["dftc%d" % T] = np.cos(a).astype(np.float32)
        c["dftns%d" % T] = (-np.sin(a)).astype(np.float32)
    dd = np.arange(128, dtype=np.float64)
    a = 2.0 * np.pi * ((dd[:, None] * dd[None, :]) % 128) / 128
    c["dftd"] = np.concatenate([np.cos(a), np.sin(a)], axis=1).astype(np.float32)
    return c


WEIGHT_NAMES = ["w_ada", "b_ada", "g_norm", "w_ffn1_in", "w_ffn1_out", "w_ffn2_in", "w_ffn2_out", "w_in",
                "b_mgate", "attn_lambda", "g_attn_sub", "g_mlstm", "w_branch_gate", "w_br_attn", "w_br_four",
                "w_br_mlstm", "w_out", "g_final"]
WEIGHT_SHAPES = {
    "w_ada": [2, 1024, 9216], "b_ada": [2, 9216], "g_norm": [2, 3, 1024], "w_ffn1_in": [2, 1024, 5632],
    "w_ffn1_out": [2, 2816, 1024], "w_ffn2_in": [2, 1024, 5632], "w_ffn2_out": [2, 2816, 1024],
    "w_in": [2, 1024, 4112], "b_mgate": [2, 16], "attn_lambda": [2, 4, 64], "g_attn_sub": [2, 128],
    "g_mlstm": [2, 128], "w_branch_gate": [2, 1024, 3072], "w_br_attn": [2, 512, 1024],
    "w_br_four": [2, 512, 1024], "w_br_mlstm": [2, 512, 1024], "w_out": [2, 1024, 1024], "g_final": [1024],
}


def build_program(stage=99, nlayers=DEPTH):
    nc = bass.Bass("TRN2", target_bir_lowering=False)
    cst = _consts()
    with ExitStack() as root:
        k = K(nc, root)

        def din(name, shape):
            return nc.dram_tensor(name, list(shape), F32, kind="ExternalInput").ap()

        def dout(name, shape):
            return nc.dram_tensor(name, list(shape), F32, kind="ExternalOutput").ap()

        xp_d = din("xp", [1024, 1024])
        xs_d = din("xs", [1024, 1024])
        ck_d = din("ck", [2, 4, 256, 128])
        cv_d = din("cv", [2, 4, 256, 128])
        sC_d = din("sC", [2, 2, 4, 128, 128])
        sn_d = din("sn", [2, 2, 4, 128])
        sm_d = din("sm", [2, 2, 4])
        cvec_d = din("cvec", [2, 1024])
        W = {n: din(n, WEIGHT_SHAPES[n]) for n in WEIGHT_NAMES}
        C = {n: din("c_" + n, list(v.shape)) for n, v in cst.items()}
        yp_d = dout("yp", [1024, 1024])
        ys_d = dout("ys", [1024, 1024])
        nk_d = dout("nk", [4, 2, 4, 256, 128])
        nv_d = dout("nv", [4, 2, 4, 256, 128])
        nC_d = dout("nC", [4, 2, 2, 4, 128, 128])
        nn_d = dout("nn", [4, 2, 2, 4, 128])
        nm_d = dout("nm", [4, 2, 2, 4])
        outs = Buf("outs", None)
        octr = [0]

        def store(out_ap, in_ap, src, sem=None, q="sp", **kw):
            if os.environ.get("DBG_NOSTORE") and out_ap.tensor.name in os.environ["DBG_NOSTORE"].split(","):
                return
            octr[0] += 1
            k.dma(q, out_ap, in_ap, src, reads=[src], writes=[(outs, octr[0])], **kw)

        dumps = {}

        def dump(name, buf, ap, shape):
            if not os.environ.get("DBG_DUMP") or name in dumps:
                return
            dumps[name] = dout("dump_" + name, shape)
            octr[0] += 1
            k.dma("pool", dumps[name], ap, buf, reads=[buf], writes=[(outs, octr[0])])

        xT = k.sb("xT", [128, 8, 2048])
        ident = k.sb("ident", [128, 128])
        ones_f = k.sb("ones_f", [128, 128])
        ones_b = k.sb("ones_b", [128, 128], BF16)
        epsc = k.sb("epsc", [128, 1])
        onec = k.sb("onec", [128, 1])
        triu_b = k.sb("triu_b", [128, 128], BF16)
        tril_b = k.sb("tril_b", [128, 128], BF16)
        mcol4 = k.sb("mcol4", [4, 4])
        sel = k.sb("sel", [4, 4, 128])
        gs = k.sb("gs", [128, 2, 3, 8, 2])
        sh = k.sb("sh", [128, 2, 3, 8, 2])
        gt = k.sb("gt", [128, 2, 3, 8, 2])
        gsubc = k.sb("gsubc", [128, 2])
        gmc = k.sb("gmc", [128, 2])
        nlam = k.sb("nlam", [128, 2])
        bmg = k.sb("bmg", [128, 2, 16])
        scr = [k.sb("scr%d" % i, [128, 512]) for i in range(8)]
        sci = [0]

        def S():
            sci[0] = (sci[0] + 1) % len(scr)
            return scr[sci[0]]

        banks = [k.ps("bank%d" % i, [128, 512]) for i in range(8)]
        bi = [0]

        nrot = [6]

        def B():
            bi[0] = (bi[0] + 1) % nrot[0]
            return banks[bi[0]]

        ACC0, ACC1 = banks[6], banks[7]
        ACCP = [(banks[6], banks[7]), (banks[4], banks[5])]

        k.dma("sp", ident[:], C["ident"][:, :], "c0", writes=[ident])
        k.dma("pool", triu_b[:], C["triu"][:, :], "c1", writes=[triu_b])
        k.dma("pool", tril_b[:], C["tril"][:, :], "c1", writes=[tril_b])
        k.dma("sp", mcol4[:], C["mcol4"][:, :], "c0", writes=[mcol4])
        k.op("dve", lambda e: e.memset(ones_f[:], 1.0), writes=[ones_f])
        k.op("dve", lambda e: e.memset(ones_b[:], 1.0), writes=[ones_b])
        k.op("dve", lambda e: e.memset(epsc[:], EPS), writes=[epsc])
        k.op("dve", lambda e: e.memset(onec[:], 1.0), writes=[onec])
        for h in range(4):
            k.op("dve", lambda e, h=h: e.tensor_scalar(sel[:, h, :], ones_f[0:4, :], mcol4[:, h:h + 1], None, op0=ALU.mult),
                 reads=[ones_f, mcol4], writes=[(sel, h)])
        k.dma("sp", bmg[:].rearrange("p l g -> p (l g)"), W["b_mgate"].rearrange("l g -> (l g)").partition_broadcast(128),
              "c0", writes=[bmg])

        def mm(out_ap, pairs, reads, writes, tr=False):
            n = len(pairs)
            fns = []
            for i, (a, b) in enumerate(pairs):
                fns.append(lambda e, a=a, b=b, i=i: e.matmul(out_ap, a, b, start=(i == 0), stop=(i == n - 1)))
            k.group("pe", fns, reads, writes)

        def mm_acc(out_ap, a, b, start, stop, reads, writes):
            k.op("pe", lambda e: e.matmul(out_ap, a, b, start=start, stop=stop), reads, writes)

        def transp(out_ap, in_ap, idn_ap, reads, writes):
            k.op("pe", lambda e: e.transpose(out_ap, in_ap, idn_ap), reads, writes)

        with k.scope():
            xin = [k.sb("xin%d" % i, [128, 1024]) for i in range(2)]
            for c in range(16):
                src = xp_d if c < 8 else xs_d
                r0 = (c % 8) * 128
                xi = xin[c % 2]
                k.dma("sp", xi[:], src[r0:r0 + 128, :], "xin%d" % (c % 2), writes=[xi])
                for g in range(2):
                    bk = B()
                    for j in range(4):
                        kc = 4 * g + j
                        transp(bk[:, j * 128:(j + 1) * 128], xi[:, kc * 128:(kc + 1) * 128], ident[:], [xi, ident], [(bk, j)])
                    k.op("act", lambda e, bk=bk, g=g, c=c: e.copy(
                        xT[:, 4 * g:4 * g + 4, c * 128:(c + 1) * 128], bk[:].rearrange("p (j t) -> p j t", j=4)),
                        reads=[bk], writes=[(xT, (4 * g + j, c // 4)) for j in range(4)])

        scT = k.sb("scT", [128, 8, 2], BF16)
        gT = k.sb("gT", [128, 6, 8])
        bT = k.sb("bT", [128, 2, 72])
        modsL = k.sb("modsL", [128, 72, 2])
        wbufs = [k.sb("wbuf%d" % i, [128, 8, 512], BF16) for i in range(3)]
        wbi = [0]
        k.nobar = set(id(b) for b in wbufs)

        pref = {}

        def wprefetch(key, parts):
            if key not in pref:
                pref[key] = wload(parts)

        def wload(parts, key=None):
            if key is not None and key in pref:
                return pref.pop(key)
            wbi[0] = (wbi[0] + 1) % len(wbufs)
            w_ = wbufs[wbi[0]]
            for (cs, ap) in parts:
                k.dma("pool", w_[:, :, cs], ap.rearrange("(k p) n -> p k n", p=128), "wb", writes=[w_])
            return w_

        bgq = []
        bgdone = set()

        def bg_step(n=1):
            for _ in range(n):
                if not bgq:
                    return
                tag, fn = bgq.pop(0)
                fn()
                bgdone.add(tag)

        def bg_ensure(tag):
            while tag not in bgdone and bgq:
                bg_step()

        def P_ffn(l, which, pc):
            w_in = W["w_ffn%d_in" % which]
            return ("ffn%d_%d_%d" % (which, l, pc),
                    [(slice(0, 256), w_in[l, :, pc * 256:(pc + 1) * 256]),
                     (slice(256, 512), w_in[l, :, DFF + pc * 256:DFF + (pc + 1) * 256])])

        def P_mg(l):
            return ("mg_%d" % l, [(slice(0, 16), W["w_in"][l, :, 4096:4112])])

        def P_mq(l, hd):
            return ("mq_%d_%d" % (l, hd),
                    [(slice(0, 128), W["w_in"][l, :, 2048 + hd * 128:2048 + (hd + 1) * 128]),
                     (slice(128, 256), W["w_in"][l, :, 2560 + hd * 128:2560 + (hd + 1) * 128]),
                     (slice(256, 384), W["w_in"][l, :, 3072 + hd * 128:3072 + (hd + 1) * 128]),
                     (slice(384, 512), W["w_in"][l, :, 3584 + hd * 128:3584 + (hd + 1) * 128])])

        def P_at(l):
            return ("at_%d" % l, [(slice(0, 512), W["w_in"][l, :, 0:512])])

        def P_fn(l):
            return ("fn_%d" % l, [(slice(0, 512), W["w_in"][l, :, 1536:2048])])

        def P_me(l, m):
            return ("me_%d_%d" % (l, m),
                    [(slice(br * 128, (br + 1) * 128), W["w_branch_gate"][l, :, br * 1024 + m * 128:br * 1024 + (m + 1) * 128])
                     for br in range(3)])

        def mods_piece(l, pc):
            w_ = wload([(slice(0, 512), W["w_ada"][l, :, pc * 512:(pc + 1) * 512])])
            bk = B()
            for jj in range(4):
                mm(bk[:, 2 * jj:2 * jj + 2], [(w_[:, kc, jj * 128:(jj + 1) * 128], scT[:, kc, :]) for kc in range(8)],
                   [w_, scT], [(bk, jj)])
            k.op("dve", lambda e: e.tensor_copy(modsL[:, pc * 4:(pc + 1) * 4, :], bk[:, 0:8].rearrange("p (j c) -> p j c", c=2)),
                 reads=[bk], writes=[(modsL, pc)])

        def mods_fin(l, i):
            for cnd in range(2):
                k.op("dve", lambda e: e.tensor_tensor(modsL[:, 24 * i:24 * i + 24, cnd], modsL[:, 24 * i:24 * i + 24, cnd],
                                                      bT[:, l, 24 * i:24 * i + 24], ALU.add), reads=[modsL, bT], writes=[modsL])
            for cnd in range(2):
                k.op("dve", lambda e: e.tensor_copy(sh[:, l, i, :, cnd], modsL[:, (3 * i) * 8:(3 * i) * 8 + 8, cnd]),
                     reads=[modsL], writes=[(sh, (l, i))])
                k.op("dve", lambda e: e.scalar_tensor_tensor(
                    gs[:, l, i, :, cnd], modsL[:, (3 * i + 1) * 8:(3 * i + 1) * 8 + 8, cnd], 1.0, gT[:, l * 3 + i, :],
                    op0=ALU.add, op1=ALU.mult), reads=[modsL, gT], writes=[(gs, (l, i))])
                k.op("dve", lambda e: e.tensor_scalar(
                    gt[:, l, i, :, cnd], modsL[:, (3 * i + 2) * 8:(3 * i + 2) * 8 + 8, cnd], (1.0 if i == 1 else 0.5), None,
                    op0=ALU.mult), reads=[modsL], writes=[(gt, (l, i))])

        if stage >= 1:
            with k.scope():
                cT = k.sb("cT", [128, 8, 2])
                alb = k.sb("alb", [128, 2, 4, 64])
                for cnd in range(2):
                    k.dma("sp", cT[:, :, cnd], cvec_d[cnd, :].rearrange("(k p) -> p k", p=128), "c0", writes=[cT],
                          allow_slow_non_contiguous=True)
                k.dma("sp", gT[:], W["g_norm"].rearrange("l i (k p) -> p (l i) k", p=128), "c0", writes=[gT],
                      allow_slow_non_contiguous=True)
                k.dma("sp", bT[:], W["b_ada"].rearrange("l (j p) -> p l j", p=128), "c0", writes=[bT],
                      allow_slow_non_contiguous=True)
                k.dma("sp", gsubc[:], W["g_attn_sub"].rearrange("l p -> p l"), "c0", writes=[gsubc], allow_slow_non_contiguous=True)
                k.dma("sp", gmc[:], W["g_mlstm"].rearrange("l p -> p l"), "c0", writes=[gmc], allow_slow_non_contiguous=True)
                k.dma("sp", alb[:].rearrange("p l a b -> p (l a b)"),
                      W["attn_lambda"].rearrange("l a b -> (l a b)").partition_broadcast(128), "c0", writes=[alb])
                k.op("act", lambda e: e.activation(scT[:], cT[:], AF.Silu), reads=[cT], writes=[scT])
                for l in range(nlayers):
                    lam_init = 0.8 - 0.6 * math.exp(-0.3 * l)
                    t1 = S(); t2 = S()
                    for q in range(2):
                        k.op("dve", lambda e, q=q: e.tensor_tensor(t1[:, 0:64], alb[:, l, 2 * q, :], alb[:, l, 2 * q + 1, :], ALU.mult),
                             reads=[alb], writes=[t1])
                        k.op("dve", lambda e, q=q: e.reduce_sum(t2[:, q:q + 1], t1[:, 0:64], AX.X), reads=[t1], writes=[t2])
                    k.op("act", lambda e: e.activation(t2[:, 2:4], t2[:, 0:2], AF.Exp), reads=[t2], writes=[t2])
                    k.op("dve", lambda e: e.tensor_tensor(t2[:, 4:5], t2[:, 3:4], t2[:, 2:3], ALU.subtract), reads=[t2], writes=[t2])
                    k.op("dve", lambda e: e.tensor_scalar(nlam[:, l:l + 1], t2[:, 4:5], -lam_init, None, op0=ALU.add),
                         reads=[t2], writes=[nlam])
                    k.op("dve", lambda e: e.tensor_scalar(gsubc[:, l:l + 1], gsubc[:, l:l + 1], 1.0 - lam_init, None, op0=ALU.mult),
                         reads=[gsubc], writes=[gsubc])
            for l in range(nlayers):
                for i in range(3):
                    for pc in range(6 * i, 6 * i + 6):
                        bgq.append((("p", l, pc), (lambda l=l, pc=pc: mods_piece(l, pc))))
                    bgq.append((("fin", l, i), (lambda l=l, i=i: mods_fin(l, i))))

        def rstd_from_psum(st_ap, width, nfeat, srcs):
            a = S()
            k.op("act", lambda e: e.activation(a[:, 0:width], st_ap, AF.Ln, bias=epsc[:], scale=1.0 / nfeat),
                 reads=srcs + [epsc], writes=[a])
            k.op("act", lambda e: e.activation(a[:, 0:width], a[:, 0:width], AF.Exp, scale=-0.5), reads=[a], writes=[a])
            return a

        pend_stat = []

        def flush_stat():
            while pend_stat:
                sq, sqb, m, t2 = pend_stat.pop(0)
                mm_acc(banks[6 + t2][:], ones_b[:], sqb[:, 0:512], m == 0, m == 7, [ones_b, sq], [banks[6 + t2]])

        PRESTAT = (stage >= 3 and DBG_BR == "mafg")
        prestat = {}

        def normmod(h, l, i, u):
            bg_ensure(("fin", l, i))
            have = prestat.pop(h, False)
            flush_stat()
            for t2 in range(2):
                tt = 2 * h + t2
                ts = slice(tt * 512, (tt + 1) * 512)
                st = banks[6 + t2] if have else B()
                for kc in range(0 if have else 8):
                    sq = S()
                    sqb = sq[:].bitcast(BF16)
                    k.op("act", lambda e, kc=kc, sqb=sqb: e.activation(sqb[:, 0:512], xT[:, kc, ts], AF.Square),
                         reads=[(xT, (kc, tt))], writes=[sq])
                    mm_acc(st[:], ones_b[:], sqb[:, 0:512], kc == 0, kc == 7, [ones_b, sq], [st])
                r = rstd_from_psum(st[:], 512, 1024.0, [st])
                for kc in range(8):
                    tm = S()
                    k.op("dve", lambda e, kc=kc, tm=tm: e.scalar_tensor_tensor(
                        tm[:], xT[:, kc, ts], gs[:, l, i, kc, h:h + 1], r[:], op0=ALU.mult, op1=ALU.mult),
                        reads=[(xT, (kc, tt)), (gs, (l, i)), r], writes=[tm])
                    k.op("act", lambda e, kc=kc, tm=tm: e.activation(
                        u[:, kc, t2 * 512:(t2 + 1) * 512], tm[:], AF.Identity, bias=sh[:, l, i, kc, h:h + 1], scale=1.0),
                        reads=[tm, (sh, (l, i))], writes=[(u, (kc, t2))])

        def resid_update(bk, l, i, h, m, t2, stat=False):
            tt = 2 * h + t2
            ts = slice(tt * 512, (tt + 1) * 512)
            k.op("dve", lambda e: e.scalar_tensor_tensor(xT[:, m, ts], bk[:], gt[:, l, i, m, h:h + 1], xT[:, m, ts],
                                                         op0=ALU.mult, op1=ALU.add),
                 reads=[bk, (gt, (l, i)), (xT, (m, tt))], writes=[(xT, (m, tt))])
            if stat:
                flush_stat()
                sq = S()
                sqb = sq[:].bitcast(BF16)
                k.op("act", lambda e: e.activation(sqb[:, 0:512], xT[:, m, ts], AF.Square), reads=[(xT, (m, tt))], writes=[sq])
                pend_stat.append((sq, sqb, m, t2))
                if m == 7 and t2 == 1:
                    prestat[h] = True

        def ffn(h, l, which, u, after=None, pre=None):
            i = 0 if which == 1 else 2
            w_in = W["w_ffn%d_in" % which]
            w_out = W["w_ffn%d_out" % which]
            with k.scope():
                hT = k.sb("hT", [128, NJ, 1024], BF16)
                wo = [k.sb("wo%d" % q, [128, NJ, 256], BF16) for q in range(2)]
                if pre is not None:
                    pre()
                normmod(h, l, i, u)
                for pc in range(11):
                    bg_step()
                    key_, parts_ = P_ffn(l, which, pc)
                    w_ = wload(parts_, key_)
                    for hc in range(2):
                        j = 2 * pc + hc
                        for t2 in range(2):
                            us = slice(t2 * 512, (t2 + 1) * 512)
                            pa = B()
                            mm(pa[:], [(w_[:, kc, hc * 128:(hc + 1) * 128], u[:, kc, us]) for kc in range(8)],
                               [w_] + [(u, (kc, t2)) for kc in range(8)], [pa])
                            pg = B()
                            mm(pg[:], [(w_[:, kc, 256 + hc * 128:256 + (hc + 1) * 128], u[:, kc, us]) for kc in range(8)],
                               [w_] + [(u, (kc, t2)) for kc in range(8)], [pg])
                            sa = S()
                            k.op("act", lambda e, pa=pa, sa=sa: e.activation(sa[:], pa[:], AF.Silu), reads=[pa], writes=[sa])
                            k.op("dve", lambda e, pg=pg, sa=sa, j=j, us=us: e.tensor_tensor(hT[:, j, us], pg[:], sa[:], ALU.mult),
                                 reads=[pg, sa], writes=[(hT, (j, t2))])
                for pc in range(4):
                    bg_step()
                    w_ = wo[pc % 2]
                    k.dma("pool", w_[:], w_out[l, :, pc * 256:(pc + 1) * 256].rearrange("(j p) n -> p j n", p=128),
                          "wo%d" % (pc % 2), writes=[w_])
                    for mc in range(2):
                        m = 2 * pc + mc
                        for t2 in range(2):
                            us = slice(t2 * 512, (t2 + 1) * 512)
                            po = B()
                            mm(po[:], [(w_[:, j, mc * 128:(mc + 1) * 128], hT[:, j, us]) for j in range(NJ)],
                               [w_] + [(hT, (j, t2)) for j in range(NJ)], [po])
                            resid_update(po, l, i, h, m, t2, stat=(PRESTAT and which == 1))
                if after is not None:
                    after()

        def proj_fm(w_, c0, u, t2, reads_extra=()):
            us = slice(t2 * 512, (t2 + 1) * 512)
            bk = B()
            mm(bk[:], [(w_[:, kc, c0:c0 + 128], u[:, kc, us]) for kc in range(8)],
               [w_] + [(u, (kc, t2)) for kc in range(8)], [bk])
            return bk

        def proj_tm(w_, c0, n, u, tc):
            t2 = tc // 4
            bk = B()
            mm(bk[:, 0:n], [(u[:, kc, tc * 128:(tc + 1) * 128], w_[:, kc, c0:c0 + n]) for kc in range(8)],
               [w_] + [(u, (kc, t2)) for kc in range(8)], [bk])
            return bk

        def feat_norm_scale(o_src, width, gcol, out_ap, out_w, srcs):
            sq = S()
            sqb = sq[:].bitcast(BF16)
            k.op("dve", lambda e: e.tensor_tensor(sqb[:, 0:width], o_src[:, 0:width], o_src[:, 0:width], ALU.mult), reads=srcs, writes=[sq])
            st = B()
            mm(st[:, 0:width], [(ones_b[:], sqb[:, 0:width])], [ones_b, sq], [st])
            r = rstd_from_psum(st[:, 0:width], width, 128.0, [st])
            dump("fn_sq", sq, sq[:], [128, 512])
            dump("fn_r", r, r[:], [128, 512])
            dump("fn_o", o_src, o_src[:], [128, 512])
            k.op("dve", lambda e: e.scalar_tensor_tensor(out_ap, o_src[:, 0:width], gcol, r[:, 0:width],
                                                         op0=ALU.mult, op1=ALU.mult),
                 reads=srcs + [r, gsubc, gmc], writes=out_w)

        def mlstm(h, l, u, moT, after=None):
            nseq, T = (4, 256) if h == 0 else (1, 1024)
            nch = T // 128
            with k.scope():
                negM = [k.sb("negM%d" % q, [4, 1024]) for q in range(2)]
                negBM = [k.sb("negBM%d" % q, [4, 1024]) for q in range(2)]
                acol = k.sb("acol", [128, 2, 8, 4])
                m1t = k.sb("m1t", [4, 2, 4])
                if h == 1:
                    m0b = k.sb("m0b", [128, 2, 4])
                    m0r = k.sb("m0r", [4, 2])
                    k.dma("sp", m0b[:].rearrange("p d g -> p (d g)"), sm_d[l].rearrange("d g -> (d g)").partition_broadcast(128),
                          "c0", writes=[m0b])
                    k.dma("sp", m0r[:], sm_d[l].rearrange("d g -> g d"), "c0", writes=[m0r], allow_slow_non_contiguous=True)
                gscope = k.scope()
                gscope.__enter__()
                gsb = k.sb("gsb", [128, 8, 16])
                G = [k.sb("G%d" % q, [4, 1024]) for q in range(4)]
                ones4 = k.sb("ones4", [4, 1024])
                k.op("dve", lambda e: e.memset(ones4[:], 1.0), writes=[ones4])
                wg = wload(P_mg(l)[1], P_mg(l)[0])
                bk = B()
                for tc in range(8):
                    mm(bk[:, tc * 16:(tc + 1) * 16], [(u[:, kc, tc * 128:(tc + 1) * 128], wg[:, kc, 0:16]) for kc in range(8)],
                       [wg] + [(u, (kc, tc // 4)) for kc in range(8)], [(bk, tc)])
                for tc in range(8):
                    k.op("dve", lambda e: e.tensor_tensor(gsb[:, tc, :], bk[:, tc * 16:(tc + 1) * 16], bmg[:, l, :], ALU.add),
                         reads=[bk, bmg], writes=[gsb])
                for ty in (1, 3):
                    gsl = gsb[:, :, 4 * ty:4 * ty + 4]
                    k.op("act", lambda e, gsl=gsl: e.activation(gsl, gsl, AF.Exp, scale=-1.0), reads=[gsb], writes=[gsb])
                    k.op("act", lambda e, gsl=gsl: e.activation(gsl, gsl, AF.Ln, bias=onec[:], scale=1.0), reads=[gsb, onec], writes=[gsb])
                    k.op("dve", lambda e, gsl=gsl: e.tensor_scalar(gsl, gsl, -1.0, None, op0=ALU.mult), reads=[gsb], writes=[gsb])
                for ty in range(4):
                    for hb in range(2):
                        bk = B()
                        for j in range(4):
                            tc = 4 * hb + j
                            transp(bk[0:4, j * 128:(j + 1) * 128], gsb[:, tc, 4 * ty:4 * ty + 4], ident[:], [gsb, ident], [(bk, j)])
                        k.op("act", lambda e, bk=bk, ty=ty, hb=hb: e.copy(G[ty][:, hb * 512:(hb + 1) * 512], bk[0:4, :]),
                             reads=[bk], writes=[G[ty]])
                for d in range(2):
                    IG, LF = G[2 * d], G[2 * d + 1]
                    for s in range(nseq):
                        def seg(tile_, lo=s * T, hi=(s + 1) * T, d=d):
                            v = tile_[:, lo:hi]
                            return v[:, ::-1] if d == 1 else v
                        tl = s * T + (T - 1 if d == 0 else 0)
                        k.op("dve", lambda e: e.tensor_tensor_scan(seg(LF), seg(ones4), seg(LF), 0.0, ALU.mult, ALU.add),
                             reads=[ones4, LF], writes=[LF])
                        k.op("dve", lambda e: e.tensor_tensor(seg(IG), seg(IG), seg(LF), ALU.subtract), reads=[IG, LF], writes=[IG])
                        init = m0r[:, d:d + 1] if h == 1 else 0.0
                        k.op("dve", lambda e: e.tensor_tensor_scan(seg(negM[d]), seg(ones4), seg(IG), init, ALU.mult, ALU.max),
                             reads=[ones4, IG] + ([m0r] if h == 1 else []), writes=[negM[d]])
                        k.op("dve", lambda e: e.tensor_tensor(seg(negBM[d]), seg(negM[d]), seg(LF), ALU.add),
                             reads=[negM[d], LF], writes=[negBM[d]])
                        if h == 0:
                            k.op("dve", lambda e: e.tensor_copy(m1t[:, d, s:s + 1], negBM[d][:, tl:tl + 1]), reads=[negBM[d]], writes=[m1t])
                    k.op("dve", lambda e: e.tensor_scalar(negM[d][:], negM[d][:], -1.0, None, op0=ALU.mult), reads=[negM[d]], writes=[negM[d]])
                    k.op("dve", lambda e: e.tensor_scalar(negBM[d][:], negBM[d][:], -1.0, None, op0=ALU.mult), reads=[negBM[d]], writes=[negBM[d]])
                    bk = B()
                    for tc in range(8):
                        transp(bk[:, tc * 4:(tc + 1) * 4], IG[:, tc * 128:(tc + 1) * 128], ident[0:4, 0:4], [IG, ident], [(bk, tc)])
                    k.op("act", lambda e, bk=bk, d=d: e.copy(acol[:, d, :, :], bk[:, 0:32].rearrange("p (c g) -> p c g", g=4)),
                         reads=[bk], writes=[acol])
                if h == 0:
                    for d in range(2):
                        store(nm_d[:, l, d, :].rearrange("s g -> g s"), m1t[:, d, :], m1t, "st_m", allow_slow_non_contiguous=True)
                gscope.__exit__(None, None, None)
                mqT = k.sb("mqT", [128, 1024], BF16)
                mkT = k.sb("mkT", [128, 1024], BF16)
                mvt = k.sb("mvt", [128, 8, 129], BF16)
                k.op("dve", lambda e: e.memset(mvt[:], 1.0), writes=[mvt])
                mkt = k.sb("mkt", [128, 8, 128], BF16)
                soT = k.sb("soT", [128, 1024])
                hs = k.sb("hs", [128, 1024])
                Wt = [k.sb("Wt%d" % q, [128, 512]) for q in range(3)]
                St = [k.sb("St%d" % q, [128, 512], BF16) for q in range(14 if h == 1 else 6)]
                Mbs = [k.sb("Mb%d" % q, [128, T]) for q in range(2)]
                Fls = [k.sb("Fl%d" % q, [128, T]) for q in range(2)]
                mbi = [0]
                si_ = [0]
                mvw = [k.sb("mvw%d" % q, [128, 129], BF16) for q in range(4)]
                cst_ = k.sb("cstage", [128, 256])
                wi_ = [0]
                if h == 1:
                    C0T = k.sb("C0T", [128, 2, 128], BF16)
                    n0c = [k.sb("n0c%d" % q, [128, 1]) for q in range(2)]
                    n0b = k.sb("n0b", [128, 2, 128], BF16)
                    c0s = k.sb("c0s", [128, 128])
                for hd in range(4):
                    wq = wload(P_mq(l, hd)[1], P_mq(l, hd)[0])
                    for t2 in range(2):
                        us = slice(t2 * 512, (t2 + 1) * 512)
                        bk = proj_fm(wq, 0, u, t2)
                        k.op("act", lambda e, bk=bk, us=us: e.copy(mqT[:, us], bk[:]), reads=[bk], writes=[(mqT, t2)])
                        bk = proj_fm(wq, 128, u, t2)
                        k.op("act", lambda e, bk=bk, us=us: e.mul(mkT[:, us], bk[:], K_SCALE), reads=[bk], writes=[(mkT, t2)])
                        bk = proj_fm(wq, 384, u, t2)
                        k.op("act", lambda e, bk=bk, us=us: e.activation(soT[:, us], bk[:], AF.Sigmoid), reads=[bk], writes=[(soT, t2)])
                    for tc in range(8):
                        if h == 0:
                            bk = proj_tm(wq, 128, 256, u, tc)
                            k.op("act", lambda e, bk=bk, tc=tc: e.mul(mkt[:, tc, :], bk[:, 0:128], K_SCALE), reads=[bk], writes=[(mkt, tc)])
                            k.op("act", lambda e, bk=bk, tc=tc: e.copy(mvt[:, tc, 0:128], bk[:, 128:256]), reads=[bk], writes=[(mvt, tc)])
                        else:
                            bk = proj_tm(wq, 256, 128, u, tc)
                            k.op("act", lambda e, bk=bk, tc=tc: e.copy(mvt[:, tc, 0:128], bk[:, 0:128]), reads=[bk], writes=[(mvt, tc)])
                    if h == 1:
                        for d in range(2):
                            k.dma("sp", c0s[:], sC_d[l, d, hd, :, :], "c0s", writes=[c0s])
                            bk = B()
                            transp(bk[:, 0:128], c0s[:], ident[:], [c0s, ident], [bk])
                            k.op("act", lambda e, bk=bk, d=d: e.copy(C0T[:, d, :], bk[:, 0:128]), reads=[bk], writes=[(C0T, d)])
                            k.dma("sp", n0c[d][:], sn_d[l, d, hd, :].rearrange("(p o) -> p o", o=1), "c0s", writes=[n0c[d]],
                                  allow_slow_non_contiguous=True)
                            k.op("dve", lambda e, d=d: e.tensor_scalar(n0b[:, d, :], ones_f[:], n0c[d][:, 0:1], None, op0=ALU.mult),
                                 reads=[ones_f, n0c[d]], writes=[(n0b, d)])
                    blocks = [(b0, min(b0 + 512, T)) for b0 in range(0, T, 512)]

                    def p1(ui, s, d, b0, b1):
                        t0 = s * T
                        tri = triu_b if d == 0 else tril_b
                        if b0 == 0:
                            mbi[0] ^= 1
                        Mb, Fl = Mbs[mbi[0]], Fls[mbi[0]]
                        if b0 == 0:
                            for (c0_, c1_) in blocks:
                                nb = c1_ - c0_
                                g0, g1 = t0 + c0_, t0 + c1_
                                eb = B()
                                mm(eb[:, 0:nb], [(sel[:, hd, :], negM[d][:, g0:g1])], [(sel, hd), negM[d]], [eb])
                                if h == 0:
                                    Mb = eb
                                else:
                                    k.op("act", lambda e: e.copy(Mb[:, c0_:c1_], eb[:, 0:nb]), reads=[eb], writes=[(Mb, c0_)])
                                fb = B()
                                mm(fb[:, 0:nb], [(sel[:, hd, :], negBM[d][:, g0:g1])], [(sel, hd), negBM[d]], [fb])
                                k.op("act", lambda e: e.activation(Fl[:, c0_:c1_], fb[:, 0:nb], AF.Exp), reads=[fb], writes=[(Fl, c0_)])
                        contrib = []
                        if h == 1:
                            contrib.append(("virt", None, b0, b1))
                        chs = list(range(nch))
                        if d == 1:
                            chs = chs[::-1]
                        for ci in chs:
                            if d == 0:
                                lo, hi = max(b0, ci * 128), b1
                            else:
                                lo, hi = b0, min(b1, ci * 128 + 128)
                            if lo < hi:
                                contrib.append(("real", ci, lo, hi))
                        ops = []
                        for idx, (kind, ci, lo, hi) in enumerate(contrib):
                            n = hi - lo
                            g0, g1 = t0 + lo, t0 + hi
                            wi_[0] = (wi_[0] + 1) % len(Wt)
                            w_t = Wt[wi_[0]]
                            si_[0] = (si_[0] + 1) % len(St)
                            s_t = St[si_[0]]
                            if kind == "virt":
                                k.op("act", lambda e: e.activation(w_t[:, 0:n], Mb[:, lo:hi], AF.Exp, bias=m0b[:, d, hd:hd + 1], scale=1.0),
                                     reads=[(Mb, b0), m0b], writes=[w_t])
                                k.op("dve", lambda e: e.tensor_tensor(s_t[:, 0:n], mqT[:, g0:g1], w_t[:, 0:n], ALU.mult),
                                     reads=[w_t, mqT], writes=[s_t])
                                lv, ld = C0T[:, d, :], n0b[:, d, :]
                                rv = [(C0T, d), (n0b, d)]
                            else:
                                c0 = t0 + ci * 128
                                tcg = c0 // 128
                                kq = B()
                                mm(kq[:, 0:n], [(mkT[:, c0:c0 + 128], mqT[:, g0:g1])], [mkT, mqT], [kq])
                                k.op("act", lambda e: e.activation(w_t[:, 0:n], Mb[:, lo:hi], AF.Exp, bias=acol[:, d, tcg, hd:hd + 1], scale=1.0),
                                     reads=[Mb if h == 0 else (Mb, b0), acol], writes=[w_t])
                                k.op("dve", lambda e: e.tensor_tensor(s_t[:, 0:n], kq[:, 0:n], w_t[:, 0:n], ALU.mult),
                                     reads=[kq, w_t], writes=[s_t])
                                if d == 0 and lo == ci * 128:
                                    k.op("dve", lambda e: e.tensor_tensor(s_t[:, 0:128], s_t[:, 0:128], tri[:], ALU.mult),
                                         reads=[s_t, tri], writes=[s_t])
                                if d == 1 and hi == ci * 128 + 128:
                                    k.op("dve", lambda e: e.tensor_tensor(s_t[:, n - 128:n], s_t[:, n - 128:n], tri[:], ALU.mult),
                                         reads=[s_t, tri], writes=[s_t])
                                lv, ld = mvt[:, tcg, 0:128], ones_b[:]
                                rv = [(mvt, tcg), ones_b]
                                if h == 0 and ((d == 0 and hi == T) or (d == 1 and lo == 0)):
                                    colw = (n - 1) if d == 0 else 0
                                    mw = mvw[(ui % 2) * 2 + ci % 2]
                                    k.op("dve", lambda e: e.tensor_scalar(mw[:, 0:129], mvt[:, tcg, 0:129], w_t[:, colw:colw + 1], None, op0=ALU.mult),
                                         reads=[(mvt, tcg), w_t], writes=[mw])
                            ops.append((lv, ld, rv, s_t, lo - b0, n))
                        return (ui, s, d, b0, b1, ops, Fl)

                    def p2(ctx):
                        ui, s, d, b0, b1, ops, Fl = ctx
                        t0 = s * T
                        nct = len(ops)
                        ACC0, ACC1 = ACCP[ui % 2]
                        for idx, (lv, ld, rv, s_t, o0, n) in enumerate(ops):
                            mm_acc(ACC0[:, o0:o0 + n], lv, s_t[:, 0:n], idx == 0, idx == nct - 1, rv + [s_t], [ACC0])
                        for idx, (lv, ld, rv, s_t, o0, n) in enumerate(ops):
                            mm_acc(ACC1[:, o0:o0 + n], ld, s_t[:, 0:n], idx == 0, idx == nct - 1, rv + [s_t], [ACC1])
                        nb = b1 - b0
                        g0, g1 = t0 + b0, t0 + b1
                        da = S()
                        k.op("act", lambda e: e.activation(da[:, 0:nb], ACC1[:, 0:nb], AF.Abs), reads=[ACC1], writes=[da])
                        k.op("dve", lambda e: e.tensor_tensor(da[:, 0:nb], da[:, 0:nb], Fl[:, b0:b1], ALU.max), reads=[da, (Fl, b0)], writes=[da])
                        k.op("act", lambda e: e.activation(da[:, 0:nb], da[:, 0:nb], AF.Ln), reads=[da], writes=[da])
                        k.op("act", lambda e: e.activation(da[:, 0:nb], da[:, 0:nb], AF.Exp, scale=-1.0), reads=[da], writes=[da])
                        if d == 0:
                            k.op("dve", lambda e: e.tensor_tensor(hs[:, g0:g1], ACC0[:, 0:nb], da[:, 0:nb], ALU.mult),
                                 reads=[ACC0, da], writes=[(hs, g0)])
                        else:
                            tm = S()
                            k.op("dve", lambda e: e.tensor_tensor(tm[:, 0:nb], ACC0[:, 0:nb], da[:, 0:nb], ALU.mult),
                                 reads=[ACC0, da], writes=[tm])
                            k.op("dve", lambda e: e.tensor_tensor(hs[:, g0:g1], hs[:, g0:g1], tm[:, 0:nb], ALU.add),
                                 reads=[(hs, g0), tm], writes=[(hs, g0)])
                        if h == 0:
                            cb = B()
                            for ci in range(nch):
                                tcg = (t0 + ci * 128) // 128
                                mw = mvw[(ui % 2) * 2 + ci % 2]
                                mm_acc(cb[:, 0:128], mw[:, 0:128], mkt[:, tcg, :], ci == 0, ci == nch - 1, [mw, (mkt, tcg)], [cb])
                            for ci in range(nch):
                                tcg = (t0 + ci * 128) // 128
                                mw = mvw[(ui % 2) * 2 + ci % 2]
                                mm_acc(cb[0:1, 128:256], mw[:, 128:129], mkt[:, tcg, :], ci == 0, ci == nch - 1, [mw, (mkt, tcg)], [cb])
                            k.op("act", lambda e: e.copy(cst_[:, 0:256], cb[:, 0:256]), reads=[cb], writes=[cst_])
                            store(nC_d[s, l, d, hd, :, :], cst_[:, 0:128], cst_, "st_c")
                            store(nn_d[s, l, d, hd, :].rearrange("(o k) -> o k", o=1), cst_[0:1, 128:256], cst_, "st_c")
                        if d == 1 and b1 == T:
                            pend_n.append(s)

                    def head_norm(s):
                        t0 = s * T
                        for (c0_, c1_) in blocks:
                            nb2 = c1_ - c0_
                            q0_, q1_ = t0 + c0_, t0 + c1_
                            ho = S()
                            k.op("act", lambda e: e.copy(ho[:, 0:nb2], hs[:, q0_:q1_]), reads=[(hs, q0_)], writes=[ho])
                            feat_norm_scale(ho, nb2, gmc[:, l:l + 1], ho[:, 0:nb2], [ho], [ho])
                            k.op("dve", lambda e: e.tensor_tensor(moT[:, hd, q0_:q1_], ho[:, 0:nb2], soT[:, q0_:q1_], ALU.mult),
                                 reads=[ho, soT], writes=[(moT, hd)])

                    pend_n = []

                    def run_p2(ctx):
                        nb4 = len(pend_n)
                        p2(ctx)
                        if nb4 > 0:
                            head_norm(pend_n.pop(0))

                    units = [(s, d, b0, b1) for s in range(nseq) for d in range(2) for (b0, b1) in blocks]
                    prev = None
                    nrot[0] = 4
                    for ui, (s, d, b0, b1) in enumerate(units):
                        ncur = (1 if h == 1 else 0) + sum(
                            1 for ci in range(nch)
                            if (max(b0, ci * 128) < b1 if d == 0 else b0 < min(b1, ci * 128 + 128)))
                        if prev is not None and len(prev[5]) + ncur > len(St):
                            run_p2(prev)
                            prev = None
                        cur = p1(ui, s, d, b0, b1)
                        if prev is not None:
                            run_p2(prev)
                        prev = cur
                    if prev is not None:
                        run_p2(prev)
                    while pend_n:
                        head_norm(pend_n.pop(0))
                    nrot[0] = 6
                if after is not None:
                    after()

        def attention(h, l, u, aoT, after=None):
            nseq, T = (4, 256) if h == 0 else (1, 1024)
            NK = 256 if h == 0 else 1280
            nkc = NK // 128
            with k.scope():
                qT = k.sb("qT", [128, 4, 1024], BF16)
                kT = k.sb("kT", [128, 4, 1280 if h == 1 else 1024], BF16)
                vt = k.sb("vt", [128, 10 if h == 1 else 8, 512], BF16)
                PTs = [k.sb("PT%d" % q, [128, 10, 512] if h == 1 else [128, 2, 256], BF16) for q in range(2)]
                ocs = [k.sb("oc%d" % q, [128, 512 if h == 1 else 256]) for q in range(2)]
                if h == 0:
                    stg = [k.sb("stg%d" % q, [128, 512]) for q in range(2)]
                rscope = k.scope()
                rscope.__enter__()
                if h == 1:
                    cosT = k.sb("cosT", [128, 1024])
                    sinT = k.sb("sinT", [128, 1024])
                    permf = k.sb("permf", [128, 128])
                    kcs = k.sb("kcs", [128, 2, 4, 128])
                    k.dma("sp", cosT[:], C["cosT"][:, :], "c0", writes=[cosT])
                    k.dma("sp", sinT[:], C["sinT"][:, :], "c0", writes=[sinT])
                    k.dma("sp", permf[:], C["perm"][:, :], "c0", writes=[permf])
                    for c in range(2):
                        k.dma("sp", kcs[:, c, :, :], ck_d[l, :, c * 128:(c + 1) * 128, :].rearrange("g p d -> p g d"), "c0", writes=[kcs])
                    for hd in range(4):
                        bk = B()
                        for c in range(2):
                            transp(bk[:, c * 128:(c + 1) * 128], kcs[:, c, hd, :], ident[:], [kcs, ident], [(bk, c)])
                        k.op("act", lambda e, bk=bk, hd=hd: e.copy(kT[:, hd, 0:256], bk[:, 0:256]), reads=[bk], writes=[(kT, hd)])
                    for c in range(2):
                        k.dma("pool", vt[:, c, :].rearrange("p (g d) -> p g d", g=4), cv_d[l, :, c * 128:(c + 1) * 128, :].rearrange("g p d -> p g d"),
                              "c1", writes=[(vt, 0), (vt, 1)])
                koff = 256 if h == 1 else 0
                if DBG_AT < -1:
                    return

                def rope_or_copy(bk, dst_ap, us, dst_w):
                    if h == 0:
                        k.op("act", lambda e: e.copy(dst_ap, bk[:]), reads=[bk], writes=dst_w)
                        return
                    q32 = S()
                    k.op("act", lambda e: e.copy(q32[:], bk[:]), reads=[bk], writes=[q32])
                    pb = B()
                    mm(pb[:], [(permf[:], q32[:])], [permf, q32], [pb])
                    t1 = S()
                    k.op("dve", lambda e: e.tensor_tensor(t1[:], q32[:], cosT[:, us], ALU.mult), reads=[q32, cosT], writes=[t1])
                    t2_ = S()
                    k.op("dve", lambda e: e.tensor_tensor(t2_[:], pb[:], sinT[:, us], ALU.mult), reads=[pb, sinT], writes=[t2_])
                    k.op("dve", lambda e: e.tensor_tensor(dst_ap, t1[:], t2_[:], ALU.add), reads=[t1, t2_], writes=dst_w)

                wq = wload(P_at(l)[1], P_at(l)[0])
                for hd in range(4):
                    for t2 in range(2):
                        us = slice(t2 * 512, (t2 + 1) * 512)
                        bk = proj_fm(wq, hd * 128, u, t2)
                        rope_or_copy(bk, qT[:, hd, us], us, [(qT, (hd, t2))])
                wk = wload([(slice(0, 512), W["w_in"][l, :, 512:1024])])
                for hd in range(4):
                    for t2 in range(2):
                        us = slice(t2 * 512, (t2 + 1) * 512)
                        bk = proj_fm(wk, hd * 128, u, t2)
                        rope_or_copy(bk, kT[:, hd, koff + t2 * 512:koff + (t2 + 1) * 512], us, [(kT, hd)])
                if DBG_AT < 0:
                    return
                if h == 0 and (DBG_TM & 1):
                    for tc in range(8):
                        bk = proj_tm(wk, 0, 512, u, tc)
                        sg_ = stg[tc % 2]
                        k.op("act", lambda e, bk=bk, sg_=sg_: e.copy(sg_[:], bk[:]), reads=[bk], writes=[sg_])
                        s, c = tc // 2, tc % 2
                        store(nk_d[s, l, :, c * 128:(c + 1) * 128, :].rearrange("g t d -> t g d"),
                              sg_[:].rearrange("p (g d) -> p g d", g=4), sg_, "st_k%d" % (tc % 2))
                wv = wload([(slice(0, 512), W["w_in"][l, :, 1024:1536])])
                for tc in range(8 if (DBG_TM & 2) else 0):
                    bk = proj_tm(wv, 0, 512, u, tc)
                    vc = tc + (2 if h == 1 else 0)
                    if h == 1:
                        k.op("act", lambda e, bk=bk, vc=vc: e.copy(vt[:, vc, :], bk[:]), reads=[bk], writes=[(vt, vc)])
                    else:
                        sg_ = stg[tc % 2]
                        k.op("act", lambda e, bk=bk, sg_=sg_: e.copy(sg_[:], bk[:]), reads=[bk], writes=[sg_])
                        k.op("dve", lambda e, sg_=sg_, vc=vc: e.tensor_copy(vt[:, vc, :], sg_[:]), reads=[sg_], writes=[(vt, vc)])
                        s, c = tc // 2, tc % 2
                        store(nv_d[s, l, :, c * 128:(c + 1) * 128, :].rearrange("g t d -> t g d"),
                              sg_[:].rearrange("p (g d) -> p g d", g=4), sg_, "st_k%d" % (tc % 2))
                rscope.__exit__(None, None, None)
                qblocks = [(s * 256, 256, s) for s in range(4)] if h == 0 else [(0, 512, 0), (512, 512, 0)]
                units = [(q0, nq, s, hd, m) for (q0, nq, s) in qblocks for hd in range(4) for m in range(2)]
                om = {}
                pend_c = []

                def stage_a(ui):
                    q0, nq, s, hd, m = units[ui]
                    PT = PTs[ui % 2]
                    ps_ = slice(64 * m, 64 * m + 64)
                    for kc in range(nkc):
                        k0 = (s * 256 if h == 0 else 0) + kc * 128
                        sb_ = B()
                        mm(sb_[:, 0:nq], [(kT[ps_, hd, k0:k0 + 128], qT[ps_, hd, q0:q0 + nq])],
                           [(kT, hd), (qT, (hd, q0 // 512))], [sb_])
                        k.op("act", lambda e: e.activation(PT[:, kc, 0:nq], sb_[:, 0:nq], AF.Exp, scale=0.125),
                             reads=[sb_], writes=[(PT, kc)])

                def stage_b(ui):
                    q0, nq, s, hd, m = units[ui]
                    PT = PTs[ui % 2]
                    vbase = (s * 2 if h == 0 else 0)
                    ob, db = ACCP[ui % 2]
                    mm(ob[:, 0:nq], [(vt[:, vbase + kc, hd * 128:(hd + 1) * 128], PT[:, kc, 0:nq]) for kc in range(nkc)],
                       [(vt, vbase + kc) for kc in range(nkc)] + [(PT, kc) for kc in range(nkc)], [ob])
                    mm(db[:, 0:nq], [(ones_b[:], PT[:, kc, 0:nq]) for kc in range(nkc)],
                       [ones_b] + [(PT, kc) for kc in range(nkc)], [db])
                    rc = S()
                    k.op("act", lambda e: e.activation(rc[:, 0:nq], db[:, 0:nq], AF.Ln), reads=[db], writes=[rc])
                    k.op("act", lambda e: e.activation(rc[:, 0:nq], rc[:, 0:nq], AF.Exp, scale=-1.0), reads=[rc], writes=[rc])
                    o_ = S()
                    k.op("dve", lambda e: e.tensor_tensor(o_[:, 0:nq], ob[:, 0:nq], rc[:, 0:nq], ALU.mult),
                         reads=[ob, rc], writes=[o_])
                    om[m] = o_
                    if m == 1:
                        oc = ocs[(ui // 2) % 2]
                        k.op("dve", lambda e: e.scalar_tensor_tensor(oc[:, 0:nq], om[1][:, 0:nq], nlam[:, l:l + 1], om[0][:, 0:nq],
                                                                     op0=ALU.mult, op1=ALU.add),
                             reads=[om[0], om[1], nlam], writes=[oc])
                        pend_c.append((oc, nq, hd, q0))

                def stage_c():
                    oc, nq, hd, q0 = pend_c.pop(0)
                    feat_norm_scale(oc, nq, gsubc[:, l:l + 1], aoT[:, hd, q0:q0 + nq], [(aoT, (hd, q0))], [oc])

                nrot[0] = 4
                stage_a(0)
                for ui in range(len(units)):
                    if ui + 1 < len(units):
                        stage_a(ui + 1)
                    if len(pend_c) > 0 and ui % 2 == 1:
                        stage_c()
                    stage_b(ui)
                while pend_c:
                    stage_c()
                nrot[0] = 6
                if after is not None:
                    after()

        def fnet(h, l, u, foT, after=None):
            nseq, T = (4, 256) if h == 0 else (1, 1024)
            nch = T // 128
            with k.scope():
                zfT = k.sb("zfT", [128, 4, 1024], BF16)
                Y = k.sb("Y", [128, 8, 4, 256], BF16)
                dftd = k.sb("dftd", [128, 256], BF16)
                dc = k.sb("dc", [128, nch, T], BF16)
                dns = k.sb("dns", [128, nch, T], BF16)
                wf = wload(P_fn(l)[1], P_fn(l)[0])
                k.dma("pool", dftd[:], C["dftd"][:, :], "c1", writes=[dftd])
                k.dma("pool", dc[:], C["dftc%d" % T].rearrange("(c p) t -> p c t", p=128), "c1", writes=[dc])
                k.dma("pool", dns[:], C["dftns%d" % T].rearrange("(c p) t -> p c t", p=128), "c1", writes=[dns])
                for g in range(4):
                    for t2 in range(2):
                        us = slice(t2 * 512, (t2 + 1) * 512)
                        bk = proj_fm(wf, g * 128, u, t2)
                        k.op("act", lambda e, bk=bk, g=g, us=us: e.copy(zfT[:, g, us], bk[:]), reads=[bk], writes=[(zfT, (g, t2))])
                for tc in range(8):
                    for gp in range(2):
                        bk = B()
                        for gg in range(2):
                            g = 2 * gp + gg
                            mm(bk[:, gg * 256:(gg + 1) * 256], [(zfT[:, g, tc * 128:(tc + 1) * 128], dftd[:])],
                               [(zfT, (g, tc // 4)), dftd], [(bk, gg)])
                        k.op("act", lambda e, bk=bk, tc=tc, gp=gp: e.copy(Y[:, tc, 2 * gp:2 * gp + 2, :], bk[:].rearrange("p (g n) -> p g n", g=2)),
                             reads=[bk], writes=[(Y, (tc, gp))])
                sc_ = 1.0 / math.sqrt(T * 128.0)
                for s in range(nseq):
                    for g in range(4):
                        for b0 in range(0, T, 512):
                            nb = min(512, T - b0)
                            pairs = []
                            rd = [dc, dns]
                            for c in range(nch):
                                tcg = s * nch + c
                                pairs.append((Y[:, tcg, g, 0:128], dc[:, c, b0:b0 + nb]))
                                pairs.append((Y[:, tcg, g, 128:256], dns[:, c, b0:b0 + nb]))
                                rd.append((Y, (tcg, g // 2)))
                            bk = B()
                            mm(bk[:, 0:nb], pairs, rd, [bk])
                            k.op("act", lambda e, bk=bk, g=g, s=s, b0=b0, nb=nb: e.mul(foT[:, g, s * T + b0:s * T + b0 + nb], bk[:, 0:nb], sc_),
                                 reads=[bk], writes=[(foT, (g, s * T + b0))])
                if after is not None:
                    after()

        def merge(h, l, u, brs, after=None):
            with k.scope():
                mg = k.sb("mg", [128, 8, 1024], BF16)
                wbr = [k.sb("wbr%d" % q, [128, 4, 3, 128], BF16) for q in range(2)]
                wnames = ["w_br_attn", "w_br_four", "w_br_mlstm"]
                for m in range(8):
                    wg = wload(P_me(l, m)[1], P_me(l, m)[0])
                    wb = wbr[m % 2]
                    for br in range(3):
                        k.dma("pool", wb[:, :, br, :], W[wnames[br]][l, :, m * 128:(m + 1) * 128].rearrange("(j p) n -> p j n", p=128),
                              "wbr%d" % (m % 2), writes=[wb])
                    for t2 in range(2):
                        us = slice(t2 * 512, (t2 + 1) * 512)
                        acc = S()
                        for br in range(3):
                            gb = B()
                            mm(gb[:], [(wg[:, kc, br * 128:(br + 1) * 128], u[:, kc, us]) for kc in range(8)],
                               [wg] + [(u, (kc, t2)) for kc in range(8)], [gb])
                            sg_ = S()
                            k.op("act", lambda e, gb=gb, sg_=sg_: e.activation(sg_[:], gb[:], AF.Sigmoid), reads=[gb], writes=[sg_])
                            bb = B()
                            mm(bb[:], [(wb[:, j, br, :], brs[br][:, j, us]) for j in range(4)], [wb, brs[br]], [bb])
                            if br == 0:
                                k.op("dve", lambda e, bb=bb, sg_=sg_: e.tensor_tensor(acc[:], bb[:], sg_[:], ALU.mult), reads=[bb, sg_], writes=[acc])
                            else:
                                tm = S()
                                k.op("dve", lambda e, bb=bb, sg_=sg_, tm=tm: e.tensor_tensor(tm[:], bb[:], sg_[:], ALU.mult), reads=[bb, sg_], writes=[tm])
                                if br == 1:
                                    k.op("dve", lambda e, tm=tm: e.tensor_tensor(acc[:], acc[:], tm[:], ALU.add), reads=[acc, tm], writes=[acc])
                                else:
                                    k.op("dve", lambda e, tm=tm: e.tensor_tensor(mg[:, m, us], acc[:], tm[:], ALU.add), reads=[acc, tm], writes=[(mg, (m, t2))])
                for pc in range(2):
                    wo_ = wload([(slice(0, 512), W["w_out"][l, :, pc * 512:(pc + 1) * 512])])
                    for mc in range(4):
                        m = 4 * pc + mc
                        for t2 in range(2):
                            us = slice(t2 * 512, (t2 + 1) * 512)
                            po = B()
                            mm(po[:], [(wo_[:, kc, mc * 128:(mc + 1) * 128], mg[:, kc, us]) for kc in range(8)],
                               [wo_] + [(mg, (kc, t2)) for kc in range(8)], [po])
                            resid_update(po, l, 1, h, m, t2, stat=PRESTAT)
                if after is not None:
                    after()

        def final_chunk(c, gfb, xo):
            dst = yp_d if c < 8 else ys_d
            r0 = (c % 8) * 128
            xo_ = xo[c % 2]
            for g in range(2):
                bk = B()
                for j in range(4):
                    kc = 4 * g + j
                    transp(bk[:, j * 128:(j + 1) * 128], xT[:, kc, c * 128:(c + 1) * 128], ident[:], [(xT, (kc, c // 4)), ident], [(bk, j)])
                k.op("act", lambda e: e.copy(xo_[:, g * 512:(g + 1) * 512], bk[:]), reads=[bk], writes=[(xo_, g)])
            sq = S(); sq2 = S(); ss = S()
            k.op("dve", lambda e: e.tensor_tensor(sq[:], xo_[:, 0:512], xo_[:, 0:512], ALU.mult), reads=[xo_], writes=[sq])
            k.op("dve", lambda e: e.tensor_tensor(sq2[:], xo_[:, 512:1024], xo_[:, 512:1024], ALU.mult), reads=[xo_], writes=[sq2])
            k.op("dve", lambda e: e.tensor_tensor(sq[:], sq[:], sq2[:], ALU.add), reads=[sq, sq2], writes=[sq])
            k.op("dve", lambda e: e.reduce_sum(ss[:, 0:1], sq[:], AX.X), reads=[sq], writes=[ss])
            k.op("act", lambda e: e.activation(ss[:, 1:2], ss[:, 0:1], AF.Sqrt, bias=epsc[:], scale=1.0 / 1024.0), reads=[ss, epsc], writes=[ss])
            k.op("dve", lambda e: e.reciprocal(ss[:, 2:3], ss[:, 1:2]), reads=[ss], writes=[ss])
            k.op("dve", lambda e: e.scalar_tensor_tensor(xo_[:], xo_[:], ss[:, 2:3], gfb[:], op0=ALU.mult, op1=ALU.mult),
                 reads=[xo_, ss, gfb], writes=[xo_])
            store(dst[r0:r0 + 128, :], xo_[:], xo_, "st_y")

        def final_alloc():
            gfb = k.sb("gfb", [128, 1024])
            xo = [k.sb("xo%d" % i, [128, 1024]) for i in range(2)]
            k.dma("sp", gfb[:], W["g_final"].partition_broadcast(128), "c0", writes=[gfb])
            return gfb, xo

        hooks = (stage >= 3 and DBG_BR == "mafg")
        early_final = [False]
        seq = [(l, h) for l in range(nlayers) for h in range(2)]
        for si, (l, h) in enumerate(seq):
            nxt = seq[si + 1] if si + 1 < len(seq) else None
            if stage >= 1:
                def a1():
                    bg_ensure(("fin", l, 1))
                    wprefetch(*P_mg(l))
                    wprefetch(*P_mq(l, 0))
                with k.scope():
                    u = k.sb("u", [128, 8, 1024], BF16)
                    ffn(h, l, 1, u, after=a1 if hooks else None)
            if stage >= 2:
                def a5():
                    bg_ensure(("fin", l, 2))
                    wprefetch(*P_ffn(l, 2, 0))
                with k.scope():
                    u = k.sb("u", [128, 8, 1024], BF16)
                    normmod(h, l, 1, u)
                    moT = k.sb("moT", [128, 4, 1024], BF16)
                    if "m" in DBG_BR:
                        mlstm(h, l, u, moT, after=(lambda: wprefetch(*P_at(l))) if hooks else None)
                    else:
                        k.op("dve", lambda e: e.memset(moT[:], 0.0), writes=[moT])
                    aoT = k.sb("aoT", [128, 4, 1024], BF16)
                    if "a" in DBG_BR:
                        attention(h, l, u, aoT, after=(lambda: wprefetch(*P_fn(l))) if hooks else None)
                    else:
                        k.op("dve", lambda e: e.memset(aoT[:], 0.0), writes=[aoT])
                    foT = k.sb("foT", [128, 4, 1024], BF16)
                    if "f" in DBG_BR:
                        fnet(h, l, u, foT, after=(lambda: wprefetch(*P_me(l, 0))) if hooks else None)
                    else:
                        k.op("dve", lambda e: e.memset(foT[:], 0.0), writes=[foT])
                    dump("aoT%d" % h, aoT, aoT[:], [128, 4, 1024])
                    dump("foT%d" % h, foT, foT[:], [128, 4, 1024])
                    dump("moT%d" % h, moT, moT[:], [128, 4, 1024])
                    if "g" in DBG_BR:
                        merge(h, l, u, [aoT, foT, moT], after=a5 if hooks else None)
            if stage >= 3:
                def a6():
                    if nxt is not None:
                        bg_ensure(("fin", nxt[0], 0))
                        wprefetch(*P_ffn(nxt[0], 1, 0))

                def pre_final():
                    gfb, xo = final_alloc()
                    for c in range(8):
                        bgq.append((("final", c), (lambda c=c: final_chunk(c, gfb, xo))))
                    early_final[0] = True
                with k.scope():
                    u = k.sb("u", [128, 8, 1024], BF16)
                    last = hooks and nxt is None and h == 1
                    ffn(h, l, 2, u, after=a6 if hooks else None, pre=pre_final if last else None)
                    if last:
                        while bgq:
                            bg_step()

        with k.scope():
            gfb, xo = final_alloc()
            for c in range(8 if early_final[0] else 0, 16):
                final_chunk(c, gfb, xo)
            deps = {}
            for st in outs.st.values():
                s_, v_ = st[0]
                if deps.get(id(s_), (None, -1))[1] < v_:
                    deps[id(s_)] = (s_, v_)
            k._emit_waits("sp", deps)
        print("program: %d instructions, %d waits, %d dma sems, sbuf high-water %d / %d" % (k.n_ins, k.n_wait, len(k.all_dsems), k.hw, nc.sbuf_top))
    return nc, cst


_CACHE = {}


def _run(inputs, stage=99, nlayers=DEPTH, ncores=NCORES):
    key = (stage, nlayers)
    if key not in _CACHE:
        _CACHE[key] = build_program(stage, nlayers)
    nc, cst = _CACHE[key]
    f = lambda a: np.ascontiguousarray(np.asarray(a, dtype=np.float32))
    wts = {n: f(inputs[n]) for n in WEIGHT_NAMES}
    csts = {"c_" + n: f(v) for n, v in cst.items()}
    xp = f(inputs["x_prompt"]); xs = f(inputs["x_sample"])
    ck = f(inputs["cache_k"]); cv = f(inputs["cache_v"])
    sC = f(inputs["state_C"]); sn = f(inputs["state_n"]); sm = f(inputs["state_m"])
    cc = f(inputs["c"]); cctx = f(inputs["c_ctx"])
    in_maps = []
    for c in range(ncores):
        b = c // 2
        m = {"xp": xp[4 * c:4 * c + 4].reshape(1024, 1024), "xs": xs[b], "ck": ck[b], "cv": cv[b], "sC": sC[b],
             "sn": sn[b], "sm": sm[b], "cvec": np.stack([cctx, cc[b]], axis=0)}
        m.update(wts)
        m.update(csts)
        in_maps.append(m)
    res = run_bass_kernel_spmd(nc, in_maps, core_ids=list(range(ncores)))
    return res.results


def kernel(**inputs):
    r = _run(inputs)
    y_prompt = np.concatenate([r[c]["yp"].reshape(4, 256, 1024) for c in range(NCORES)], axis=0)
    y_sample = np.stack([r[2 * b]["ys"] for b in range(4)], axis=0)
    nk = np.concatenate([r[c]["nk"] for c in range(NCORES)], axis=0)
    nv = np.concatenate([r[c]["nv"] for c in range(NCORES)], axis=0)
    nC = np.concatenate([r[c]["nC"] for c in range(NCORES)], axis=0)
    nn = np.concatenate([r[c]["nn"] for c in range(NCORES)], axis=0)
    nm = np.concatenate([r[c]["nm"] for c in range(NCORES)], axis=0)
    return (y_prompt.astype(np.float32), y_sample.astype(np.float32), nk.astype(np.float32), nv.astype(np.float32),
            nC.astype(np.float32), nn.astype(np.float32), nm.astype(np.float32))
```
